# Optimizing a Trainium2 kernel written in Bass

```python
import jax, jax.numpy as jnp
from jax import lax
import numpy as np

D_MODEL = 1024
BATCH = 16
SEQ = 256
DEPTH = 4
DEC_BATCH = 4
DEC_SEQ = 4096
PAST_LEN = 256

GRID_W = 64
N_MIXERS = 3
N_A = (DEPTH + 2) // 3
N_B = (DEPTH + 1) // 3
N_C = DEPTH // 3
N_HEADS = 16
KV_HEADS = 4
GROUP = N_HEADS // KV_HEADS
HEAD_DIM = 64
QKV_DIM = (N_HEADS + 2 * KV_HEADS) * HEAD_DIM
WINDOW = 128
BLOCK = 128
Q_LORA = 384
KV_LORA = 256
NOPE_DIM = 64
ROPE_DIM = 32
V_DIM = 64
D_FF = 4 * D_MODEL
ROPE_THETA = 10000.0
EPS = 1e-6
NEG_INF = -1e30
ATTN_SCALE = HEAD_DIM ** -0.5
MLA_SCALE = (NOPE_DIM + ROPE_DIM) ** -0.5

kernel_name = 'hybrid_diffusion_prefix_trunk_step'


def rmsnorm(x, g):
    xf = x.astype(jnp.float32)
    y = xf * lax.rsqrt(jnp.mean(xf * xf, axis=-1, keepdims=True) + EPS)
    return (y * g.astype(jnp.float32)).astype(x.dtype)


def grid_positions(t_len):
    n_rows = t_len // GRID_W
    rows = jnp.repeat(jnp.arange(n_rows), GRID_W).astype(jnp.float32)
    cols = jnp.tile(jnp.arange(GRID_W), n_rows).astype(jnp.float32)
    return rows, cols


def _rope_1d(x, pos):
    d = x.shape[-1]
    freqs = ROPE_THETA ** (-jnp.arange(0, d, 2, dtype=jnp.float32) / d)
    ang = pos[:, None] * freqs[None, :]
    cos = jnp.cos(ang)[:, None, :].astype(x.dtype)
    sin = jnp.sin(ang)[:, None, :].astype(x.dtype)
    x1, x2 = jnp.split(x, 2, axis=-1)
    return jnp.concatenate([x1 * cos - x2 * sin, x1 * sin + x2 * cos], axis=-1)


def rope_2d(x, rows, cols):
    half = x.shape[-1] // 2
    return jnp.concatenate([_rope_1d(x[..., :half], rows), _rope_1d(x[..., half:], cols)], axis=-1)


def ada_mod(cond, w, b):
    m = jax.nn.silu(cond) @ w + b
    return jnp.split(m[..., None, :], 6, axis=-1)


def modulate(h, shift, scale):
    return h * (1.0 + scale) + shift


def attend(q, k, v, scale, sink=None, mask=None):
    s = jnp.einsum('bqhgd,bkhd->bhgqk', q, k, preferred_element_type=jnp.float32) * scale
    if mask is not None:
        s = jnp.where(mask, s, NEG_INF)
    if sink is not None:
        sink_col = jnp.broadcast_to(sink.astype(jnp.float32)[None, :, :, None, None], s.shape[:-1] + (1,))
        p = jax.nn.softmax(jnp.concatenate([sink_col, s], axis=-1), axis=-1)[..., 1:]
    else:
        p = jax.nn.softmax(s, axis=-1)
    return jnp.einsum('bhgqk,bkhd->bqhgd', p.astype(v.dtype), v)


def merge_blocks(o):
    o = jnp.moveaxis(o, 0, 1)
    return o.reshape((o.shape[0], o.shape[1] * o.shape[2]) + o.shape[3:])


def blocked_attend(q, k, v, scale, sink=None):
    t_len = q.shape[1]

    def one(b):
        qb = lax.dynamic_slice_in_dim(q, b * BLOCK, BLOCK, axis=1)
        return attend(qb, k, v, scale, sink)

    return merge_blocks(lax.map(one, jnp.arange(t_len // BLOCK)))


def split_qkv(qkv):
    b, t, _ = qkv.shape
    q, k, v = jnp.split(qkv, [N_HEADS * HEAD_DIM, (N_HEADS + KV_HEADS) * HEAD_DIM], axis=-1)
    return (q.reshape(b, t, N_HEADS, HEAD_DIM), k.reshape(b, t, KV_HEADS, HEAD_DIM),
            v.reshape(b, t, KV_HEADS, HEAD_DIM))


def mixer_a_context(h, w_qkv, sink, w_o):
    b, s, _ = h.shape
    q, k, v = split_qkv(h @ w_qkv)
    o = blocked_attend(q.reshape(b, s, KV_HEADS, GROUP, HEAD_DIM), k, v, ATTN_SCALE, sink.reshape(KV_HEADS, GROUP))
    return o.reshape(b, s, N_HEADS * HEAD_DIM) @ w_o, k, v


def mixer_a_latent(h, ctx_k, ctx_v, w_qkv, sink, w_o, rows, cols):
    b, t, _ = h.shape
    p_len = ctx_k.shape[1]
    q, k, v = split_qkv(h @ w_qkv)
    q = rope_2d(q, rows, cols).reshape(b, t, KV_HEADS, GROUP, HEAD_DIM)
    k = rope_2d(k, rows, cols)
    pad = ((0, 0), (BLOCK, BLOCK), (0, 0), (0, 0))
    kp = jnp.pad(k, pad)
    vp = jnp.pad(v, pad)
    q_off = jnp.arange(BLOCK)[:, None]
    k_off = jnp.arange(3 * BLOCK)[None, :] - BLOCK
    ctx_mask = jnp.ones((BLOCK, p_len), dtype=bool)
    sink_g = sink.reshape(KV_HEADS, GROUP)

    def one(blk):
        start = blk * BLOCK
        qb = lax.dynamic_slice_in_dim(q, start, BLOCK, axis=1)
        kb = lax.dynamic_slice_in_dim(kp, start, 3 * BLOCK, axis=1)
        vb = lax.dynamic_slice_in_dim(vp, start, 3 * BLOCK, axis=1)
        kpos = start + k_off
        win = (jnp.abs(q_off - k_off) <= WINDOW) & (kpos >= 0) & (kpos < t)
        mask = jnp.concatenate([ctx_mask, win], axis=1)
        return attend(qb, jnp.concatenate([ctx_k, kb], axis=1), jnp.concatenate([ctx_v, vb], axis=1),
                      ATTN_SCALE, sink_g, mask)

    o = merge_blocks(lax.map(one, jnp.arange(t // BLOCK)))
    return o.reshape(b, t, N_HEADS * HEAD_DIM) @ w_o


def mla_project(h, w_dq, g_q, w_uq, w_dkv, g_kv):
    b, t, _ = h.shape
    q = (rmsnorm(h @ w_dq, g_q) @ w_uq).reshape(b, t, N_HEADS, NOPE_DIM + ROPE_DIM)
    ckv = h @ w_dkv
    c_kv = rmsnorm(ckv[..., :KV_LORA], g_kv)
    k_rope = ckv[..., KV_LORA:]
    return q, c_kv, k_rope


def mla_expand(c_kv, k_rope, w_ukv):
    b, t, _ = c_kv.shape
    kv = (c_kv @ w_ukv).reshape(b, t, N_HEADS, NOPE_DIM + V_DIM)
    k_nope, v = kv[..., :NOPE_DIM], kv[..., NOPE_DIM:]
    k = jnp.concatenate([k_nope, jnp.broadcast_to(k_rope[:, :, None, :], (b, t, N_HEADS, ROPE_DIM))], axis=-1)
    return k, v


def mixer_b_context(h, w_dq, g_q, w_uq, w_dkv, g_kv, w_ukv, w_o):
    b, s, _ = h.shape
    q, c_kv, k_rope = mla_project(h, w_dq, g_q, w_uq, w_dkv, g_kv)
    k, v = mla_expand(c_kv, k_rope, w_ukv)
    o = blocked_attend(q[:, :, :, None, :], k, v, MLA_SCALE)
    return o.reshape(b, s, N_HEADS * V_DIM) @ w_o, c_kv, k_rope


def mixer_b_latent(h, ctx_ckv, ctx_krope, w_dq, g_q, w_uq, w_dkv, g_kv, w_ukv, w_o, rows, cols):
    b, t, _ = h.shape
    q, c_kv, k_rope = mla_project(h, w_dq, g_q, w_uq, w_dkv, g_kv)
    q = jnp.concatenate([q[..., :NOPE_DIM], rope_2d(q[..., NOPE_DIM:], rows, cols)], axis=-1)
    k_rope = rope_2d(k_rope[:, :, None, :], rows, cols)[:, :, 0, :]
    k_lat, v_lat = mla_expand(c_kv, k_rope, w_ukv)
    k_ctx, v_ctx = mla_expand(ctx_ckv, ctx_krope, w_ukv)
    k_all = jnp.concatenate([k_ctx, k_lat], axis=1)
    v_all = jnp.concatenate([v_ctx, v_lat], axis=1)
    o = blocked_attend(q[:, :, :, None, :], k_all, v_all, MLA_SCALE)
    return o.reshape(b, t, N_HEADS * V_DIM) @ w_o


def mixer_c_context(h, w_qkv, g_q, g_k, w_o):
    b, s, _ = h.shape
    q, k, v = split_qkv(h @ w_qkv)
    q = rmsnorm(q, g_q)
    k = rmsnorm(k, g_k)
    o = blocked_attend(q.reshape(b, s, KV_HEADS, GROUP, HEAD_DIM), k, v, ATTN_SCALE)
    return o.reshape(b, s, N_HEADS * HEAD_DIM) @ w_o, k, v


def mixer_c_latent(h, ctx_k, ctx_v, w_qkv, g_q, g_k, w_o, rows, cols):
    b, t, _ = h.shape
    q, k, v = split_qkv(h @ w_qkv)
    q = rope_2d(rmsnorm(q, g_q), rows, cols).reshape(b, t, KV_HEADS, GROUP, HEAD_DIM)
    k = rope_2d(rmsnorm(k, g_k), rows, cols)
    k_all = jnp.concatenate([ctx_k, k], axis=1)
    v_all = jnp.concatenate([ctx_v, v], axis=1)
    o = blocked_attend(q, k_all, v_all, ATTN_SCALE)
    return o.reshape(b, t, N_HEADS * HEAD_DIM) @ w_o


def sq_relu_mlp(h, w_in, w_out):
    return jnp.square(jax.nn.relu(h @ w_in)) @ w_out


def setup_inputs(seed: int = 0) -> dict:
    key = jax.random.key(seed)
    ks = jax.random.split(key, 32)

    def nrm(k, shape, scale=1.0):
        return jax.random.normal(k, shape, jnp.float32) * scale

    def gain(k, shape):
        return 1.0 + 0.02 * jax.random.normal(k, shape, jnp.float32)

    d = D_MODEL
    return {
        'x_prompt': nrm(ks[0], (BATCH, SEQ, d)),
        'x_sample': nrm(ks[1], (DEC_BATCH, DEC_SEQ, d)),
        'c': nrm(ks[2], (DEC_BATCH, d)),
        'cache_a_k': nrm(ks[3], (DEC_BATCH, N_A, PAST_LEN, KV_HEADS, HEAD_DIM)),
        'cache_a_v': nrm(ks[4], (DEC_BATCH, N_A, PAST_LEN, KV_HEADS, HEAD_DIM)),
        'cache_b_ckv': nrm(ks[5], (DEC_BATCH, N_B, PAST_LEN, KV_LORA)),
        'cache_b_krope': nrm(ks[6], (DEC_BATCH, N_B, PAST_LEN, ROPE_DIM)),
        'cache_c_k': nrm(ks[7], (DEC_BATCH, N_C, PAST_LEN, KV_HEADS, HEAD_DIM)),
        'cache_c_v': nrm(ks[8], (DEC_BATCH, N_C, PAST_LEN, KV_HEADS, HEAD_DIM)),
        'c_ctx': nrm(ks[9], (d,)),
        'w_ada': nrm(ks[10], (DEPTH, d, 6 * d), 0.5 * d ** -0.5),
        'b_ada': nrm(ks[11], (DEPTH, 6 * d), 0.02),
        'norm_g': gain(ks[12], (DEPTH, 2, d)),
        'w_mlp_in': nrm(ks[13], (DEPTH, d, D_FF), d ** -0.5),
        'w_mlp_out': nrm(ks[14], (DEPTH, D_FF, d), D_FF ** -0.5),
        'a_w_qkv': nrm(ks[15], (N_A, d, QKV_DIM), d ** -0.5),
        'a_sink': nrm(ks[16], (N_A, N_HEADS), 0.5),
        'a_w_o': nrm(ks[17], (N_A, N_HEADS * HEAD_DIM, d), (N_HEADS * HEAD_DIM) ** -0.5),
        'b_w_dq': nrm(ks[18], (N_B, d, Q_LORA), d ** -0.5),
        'b_g_q': gain(ks[19], (N_B, Q_LORA)),
        'b_w_uq': nrm(ks[20], (N_B, Q_LORA, N_HEADS * (NOPE_DIM + ROPE_DIM)), Q_LORA ** -0.5),
        'b_w_dkv': nrm(ks[21], (N_B, d, KV_LORA + ROPE_DIM), d ** -0.5),
        'b_g_kv': gain(ks[22], (N_B, KV_LORA)),
        'b_w_ukv': nrm(ks[23], (N_B, KV_LORA, N_HEADS * (NOPE_DIM + V_DIM)), KV_LORA ** -0.5),
        'b_w_o': nrm(ks[24], (N_B, N_HEADS * V_DIM, d), (N_HEADS * V_DIM) ** -0.5),
        'c_w_qkv': nrm(ks[25], (N_C, d, QKV_DIM), d ** -0.5),
        'c_g_q': gain(ks[26], (N_C, HEAD_DIM)),
        'c_g_k': gain(ks[27], (N_C, HEAD_DIM)),
        'c_w_o': nrm(ks[28], (N_C, N_HEADS * HEAD_DIM, d), (N_HEADS * HEAD_DIM) ** -0.5),
        'g_final': gain(ks[29], (d,)),
    }


def reference(x_prompt, x_sample, c, cache_a_k, cache_a_v, cache_b_ckv, cache_b_krope, cache_c_k, cache_c_v,
              c_ctx, w_ada, b_ada, norm_g, w_mlp_in, w_mlp_out,
              a_w_qkv, a_sink, a_w_o,
              b_w_dq, b_g_q, b_w_uq, b_w_dkv, b_g_kv, b_w_ukv, b_w_o,
              c_w_qkv, c_g_q, c_g_k, c_w_o, g_final):
    rows, cols = grid_positions(x_sample.shape[1])
    xp, xs = x_prompt, x_sample
    st_a_k, st_a_v, st_b_ckv, st_b_kr, st_c_k, st_c_v = [], [], [], [], [], []
    for i in range(DEPTH):
        kind = i % N_MIXERS
        j = i // N_MIXERS
        sh_p, sc_p, gt_p, sh2_p, sc2_p, gt2_p = ada_mod(c_ctx, w_ada[i], b_ada[i])
        sh_s, sc_s, gt_s, sh2_s, sc2_s, gt2_s = ada_mod(c, w_ada[i], b_ada[i])
        hp = modulate(rmsnorm(xp, norm_g[i, 0]), sh_p, sc_p)
        hs = modulate(rmsnorm(xs, norm_g[i, 0]), sh_s, sc_s)
        if kind == 0:
            op, k_ctx, v_ctx = mixer_a_context(hp, a_w_qkv[j], a_sink[j], a_w_o[j])
            st_a_k.append(k_ctx)
            st_a_v.append(v_ctx)
            os_ = mixer_a_latent(hs, cache_a_k[:, j], cache_a_v[:, j], a_w_qkv[j], a_sink[j], a_w_o[j], rows, cols)
        elif kind == 1:
            op, ckv_ctx, kr_ctx = mixer_b_context(hp, b_w_dq[j], b_g_q[j], b_w_uq[j], b_w_dkv[j], b_g_kv[j],
                                                  b_w_ukv[j], b_w_o[j])
            st_b_ckv.append(ckv_ctx)
            st_b_kr.append(kr_ctx)
            os_ = mixer_b_latent(hs, cache_b_ckv[:, j], cache_b_krope[:, j], b_w_dq[j], b_g_q[j], b_w_uq[j],
                                 b_w_dkv[j], b_g_kv[j], b_w_ukv[j], b_w_o[j], rows, cols)
        else:
            op, k_ctx, v_ctx = mixer_c_context(hp, c_w_qkv[j], c_g_q[j], c_g_k[j], c_w_o[j])
            st_c_k.append(k_ctx)
            st_c_v.append(v_ctx)
            os_ = mixer_c_latent(hs, cache_c_k[:, j], cache_c_v[:, j], c_w_qkv[j], c_g_q[j], c_g_k[j], c_w_o[j],
                                 rows, cols)
        xp = xp + gt_p * op
        xs = xs + gt_s * os_
        hp = modulate(rmsnorm(xp, norm_g[i, 1]), sh2_p, sc2_p)
        hs = modulate(rmsnorm(xs, norm_g[i, 1]), sh2_s, sc2_s)
        xp = xp + gt2_p * sq_relu_mlp(hp, w_mlp_in[i], w_mlp_out[i])
        xs = xs + gt2_s * sq_relu_mlp(hs, w_mlp_in[i], w_mlp_out[i])
    y_prompt = rmsnorm(xp, g_final)
    y_sample = rmsnorm(xs, g_final)
    state_a_k = jnp.stack(st_a_k, axis=1)
    state_a_v = jnp.stack(st_a_v, axis=1)
    state_b_ckv = jnp.stack(st_b_ckv, axis=1)
    state_b_krope = jnp.stack(st_b_kr, axis=1)
    state_c_k = jnp.stack(st_c_k, axis=1)
    state_c_v = jnp.stack(st_c_v, axis=1)
    return (y_prompt, y_sample, state_a_k, state_a_v, state_b_ckv, state_b_krope, state_c_k, state_c_v)
```

```python
import numpy as np
from contextlib import ExitStack
import concourse.bass as bass
import concourse.mybir as mybir
from concourse.bass_utils import run_bass_kernel_spmd

F32, BF = mybir.dt.float32, mybir.dt.bfloat16
ALU = mybir.AluOpType
AF = mybir.ActivationFunctionType

D = 1024
NT = 5
T = 2560
TS = 2048
EPS = 1e-6
THETA = 10000.0
DEPTH = 4
SAME_ENGINE_SYNC = True
CH = 4000


class Op:
    pass


class Prog:
    ENGS = ('pe', 'act', 'dve', 'pool', 'sp')

    def __init__(self):
        self.ops = []

    def add(self, eng, fn, reads=(), writes=(), kind='c', stream=None):
        o = Op()
        reads, writes = tuple(reads), tuple(writes)
        writes = writes + tuple(r for r in reads if isinstance(r, tuple) and r[0] == 'ps' and r not in writes)
        o.eng, o.fn, o.reads, o.writes, o.kind, o.stream = eng, fn, reads, writes, kind, stream
        o.sig = False
        self.ops.append(o)
        return o

    def barrier(self):
        for e in self.ENGS:
            self.add(e, None, kind='bar')

    def analyze(self):
        last_w, readers = {}, {}
        eng_ops = {e: [] for e in self.ENGS}
        stream_cnt, stream_last = {}, {}
        last_c = {}
        for o in self.ops:
            deps = {}

            def add(d, ty):
                if d is not o:
                    deps.setdefault(d, set()).add(ty)
            if o.kind == 'bar':
                for e in self.ENGS:
                    if e != o.eng and e in last_c:
                        add(last_c[e], 'RAW')
                for s, d in stream_last.items():
                    add(d, 'RAW')
            else:
                for r in o.reads:
                    if r in last_w:
                        add(last_w[r], 'RAW')
                for w in o.writes:
                    if w in last_w:
                        add(last_w[w], 'WAW')
                    for rd in readers.get(w, ()):
                        add(rd, 'WAR')
            if o.kind in ('dma', 'cc'):
                prev = stream_last.get(o.stream)
                if prev is not None:
                    add(prev, 'RAW')
                stream_cnt[o.stream] = stream_cnt.get(o.stream, 0) + 1
                o.dman = stream_cnt[o.stream]
                stream_last[o.stream] = o
            o.deps = deps
            for r in o.reads:
                readers.setdefault(r, []).append(o)
            for w in o.writes:
                last_w[w] = o
                readers[w] = []
            if o.kind == 'c':
                last_c[o.eng] = o
            o.eidx = len(eng_ops[o.eng])
            eng_ops[o.eng].append(o)
        self.eng_ops = eng_ops
        self.stream_cnt = stream_cnt
        waited = {e: {d: -1 for d in self.ENGS} for e in self.ENGS}
        waited_s = {e: {} for e in self.ENGS}
        for o in self.ops:
            E = o.eng
            cw = {}
            sw = {}
            for d, tys in o.deps.items():
                if d.kind in ('dma', 'cc'):
                    if d.dman > sw.get(d.stream, (0, None))[0]:
                        sw[d.stream] = (d.dman, d)
                elif d.kind == 'c':
                    if d.eng == E and o.kind == 'c':
                        if E == 'pe' or not SAME_ENGINE_SYNC or not (tys & {'RAW', 'WAW'}):
                            continue
                    if d.eng not in cw or d.eidx > cw[d.eng].eidx:
                        cw[d.eng] = d
            o.cwaits, o.swaits = [], []
            for De, d in cw.items():
                if d.eidx <= waited[E][De]:
                    continue
                if De == E and o.kind == 'c':
                    if any(x.eng != E and x.kind == 'c' and d in x.deps for x in cw.values()):
                        continue
                waited[E][De] = d.eidx
                d.sig = True
                o.cwaits.append(d)
            for s, (n, d) in sw.items():
                if n <= waited_s[E].get(s, 0):
                    continue
                waited_s[E][s] = n
                o.swaits.append(d)
        self.nsig = {}
        for e in self.ENGS:
            k = 0
            for o in eng_ops[e]:
                if o.sig:
                    k += 1
                    o.sigk = k
            self.nsig[e] = k


def build_program(nlayers=DEPTH, stop=9):
    nc = bass.Bass("TRN2", target_bir_lowering=False)
    P = Prog()
    es = ExitStack()

    def din(name, shape, dt=F32):
        return nc.dram_tensor(name, list(shape), dt, kind="ExternalInput").ap()

    def dout(name, shape, dt=F32):
        return nc.dram_tensor(name, list(shape), dt, kind="ExternalOutput").ap()

    xT = din("xT", [D, T])
    cvec = din("cvec", [128, 8, 2])
    rope = din("rope", [4, 128, TS])
    masks = din("masks", [128, 4, 128])
    consts = din("consts", [128, 6, 128])
    w_ada = din("w_ada", [nlayers, D, 6 * D])
    b_adaT = din("b_adaT", [128, DEPTH, 48])
    ngT = din("ngT", [128, DEPTH * 2 + 1, 8])
    w_mlp_in = din("w_mlp_in", [nlayers, D, 4 * D])
    w_mlp_out = din("w_mlp_out", [nlayers, 4 * D, D])
    a_w_qkv = din("a_w_qkv", [2, D, 1536])
    a_w_o = din("a_w_o", [2, D, D])
    a_sinkb = din("a_sinkb", [128, 2, 16])
    b_w_dq = din("b_w_dq", [D, 384])
    b_w_uq = din("b_w_uq", [384, 1536])
    b_w_dkv = din("b_w_dkv", [D, 288])
    b_w_ukv = din("b_w_ukv", [256, 2048])
    b_w_o = din("b_w_o", [D, D])
    b_gT = din("b_gT", [128, 5])
    c_w_qkv = din("c_w_qkv", [D, 1536])
    c_w_o = din("c_w_o", [D, D])
    c_gT = din("c_gT", [128, 2])
    ctx_a_k = din("ctx_a_k", [2, 256, 256])
    ctx_a_v = din("ctx_a_v", [2, 256, 256])
    ctx_b_ckv = din("ctx_b_ckv", [256, 256])
    ctx_b_kr = din("ctx_b_kr", [32, 256])
    ctx_c_k = din("ctx_c_k", [256, 256])
    ctx_c_v = din("ctx_c_v", [256, 256])

    yT = dout("yT", [D, T])
    st_a_k = dout("st_a_k", [2, 256, 512])
    st_a_v = dout("st_a_v", [2, 512, 256])
    st_b_ckv = dout("st_b_ckv", [256, 512])
    st_b_kr = dout("st_b_kr", [32, 512])
    st_c_k = dout("st_c_k", [256, 512])
    st_c_v = dout("st_c_v", [512, 256])

    def dint(name, shape):
        return nc.dram_tensor(name, list(shape), BF)
    exAkI = [dint(f"exAki{j}", [256, 256]) for j in range(2)]
    exAkO = [dint(f"exAko{j}", [512, 256]) for j in range(2)]
    exAvI = [dint(f"exAvi{j}", [256, 256]) for j in range(2)]
    exAvO = [dint(f"exAvo{j}", [512, 256]) for j in range(2)]
    exK_in = {l: dint(f"exKi{l}", [256, TS]) for l in ('B', 'C')}
    exK_out = {l: dint(f"exKo{l}", [512, TS]) for l in ('B', 'C')}
    exV_in = dint("exVi", [TS, 256])
    exV_out = dint("exVo", [2 * TS, 256])
    exR_in = dint("exRi", [32, TS])
    exR_out = dint("exRo", [64, TS])

    def sb(name, shape, dt):
        return es.enter_context(nc.sbuf_tensor(name, list(shape), dt))
    X = sb("X", [128, 8, T], F32)
    H = sb("H", [128, 8, T], BF)
    SCR = sb("SCR", [128, 4, 512], F32)
    SQB = sb("SQB", [128, 2, 512], BF)
    MODSL = [sb("MODS0", [128, 2, 64], F32), sb("MODS1", [128, 2, 64], F32)]
    CB = sb("CB", [128, 6, 128], BF)
    ONESB = sb("ONESB", [128, 128], BF)
    ONESF = sb("ONESF", [128, 128], F32)
    MASK = sb("MASK", [128, 4, 128], BF)
    NG = sb("NG", [128, DEPTH * 2 + 1, 8], F32)
    BADA = sb("BADA", [128, DEPTH, 48], F32)
    BG = sb("BG", [128, 5], F32)
    CG = sb("CG", [128, 2], F32)
    ESINK = sb("ESINK", [128, 2, 16], F32)
    SILU = sb("SILU", [128, 8, 2], BF)
    CV = sb("CV", [128, 8, 2], F32)
    EPSC = sb("EPSC", [128, 1], F32)
    ZEROC = sb("ZEROC", [128, 1], F32)
    ROPE = sb("ROPE", [128, 2, 512], F32)
    ARENA_N = 34304
    ARENA = sb("ARENA", [128, ARENA_N], BF)
    ps = [es.enter_context(nc.psum_tensor(f"ps{i}", [128, 512], F32)) for i in range(8)]

    IDENT = CB[:, 0, :]
    PERM64 = CB[:, 1, :]
    PERMM = CB[:, 2, :]
    BDIAG = CB[:, 3, :]

    def arena(off, n):
        assert off + n <= ARENA_N, (off, n)
        return ARENA[:, off:off + n]

    wq_ctr = [0]
    NW = 3

    def dma(queue, out, in_, reads, writes, stream):
        P.add(queue, lambda e, o=out, i=in_: e.dma_start(out=o, in_=i), reads, writes, kind='dma', stream=queue + '_' + stream)

    io_ctr = [0]

    def io_stream():
        io_ctr[0] += 1
        return f"io{io_ctr[0] % 6}"

    def mm(out, lhsT, rhs, start, stop, reads, writes, skip=False):
        if skip:
            P.add('pe', lambda e, o=out, l=lhsT, r=rhs, s=start, t=stop: e.matmul(o, l, r, start=s, stop=t, skip_group_check=True), reads, writes)
        else:
            P.add('pe', lambda e, o=out, l=lhsT, r=rhs, s=start, t=stop: e.matmul(o, l, r, start=s, stop=t), reads, writes)

    def act(out, in_, func, reads, writes, bias=None, scale=None):
        kw = {}
        if bias is None:
            p0 = out.base_partition()
            bias = ZEROC[p0:p0 + out.shape[0], 0:1]
        kw['bias'] = bias
        if scale is not None:
            kw['scale'] = scale
        P.add('act', lambda e, o=out, i=in_, f=func, k=kw: e.activation(out=o, in_=i, func=f, **k), reads, writes)

    def tt(eng, out, in0, in1, op, reads, writes):
        P.add(eng, lambda e, o=out, a=in0, b=in1, p=op: e.tensor_tensor(out=o, in0=a, in1=b, op=p), reads, writes)

    def stt(eng, out, in0, scalar, in1, op0, op1, reads, writes):
        P.add(eng, lambda e, o=out, a=in0, s=scalar, b=in1, p=op0, q=op1:
              e.scalar_tensor_tensor(out=o, in0=a, scalar=s, in1=b, op0=p, op1=q), reads, writes)

    def ts(eng, out, in0, s1, s2, op0, op1, reads, writes):
        if s2 is None:
            P.add(eng, lambda e, o=out, a=in0, s=s1, p=op0: e.tensor_scalar(out=o, in0=a, scalar1=s, scalar2=None, op0=p), reads, writes)
        else:
            P.add(eng, lambda e, o=out, a=in0, s=s1, u=s2, p=op0, q=op1:
                  e.tensor_scalar(out=o, in0=a, scalar1=s, scalar2=u, op0=p, op1=q), reads, writes)

    def recip(out, in_, reads, writes):
        P.add('dve', lambda e, o=out, i=in_: e.reciprocal(out=o, in_=i), reads, writes)

    def cp(eng, out, in_, reads, writes):
        if eng == 'act':
            act(out, in_, AF.Identity, reads, writes)
        else:
            P.add(eng, lambda e, o=out, i=in_: e.tensor_copy(out=o, in_=i), reads, writes)

    def memset(eng, ap, val, writes):
        P.add(eng, lambda e, a=ap, v=val: e.memset(a, v), (), writes)

    def tsl(t):
        return slice(t * 512, (t + 1) * 512)

    rot = {}

    def rotate(name, banks):
        i = rot.get(name, 0)
        rot[name] = i + 1
        return banks[i % len(banks)]

    def scr(i):
        return SCR[:, i, :]

    def wload(view, src, tag):
        wq_ctr[0] += 1
        dma('pool', view, src.rearrange("(k p) n -> p k n", p=128), (), (tag,), f"w{wq_ctr[0] % 6}")

    memset('dve', ONESB[:], 1.0, ('ONESB',))
    memset('dve', ONESF[:], 1.0, ('ONESF',))
    memset('dve', EPSC[:], EPS, ('EPSC',))
    memset('dve', ZEROC[:], 0.0, ('ZEROC',))
    for k in range(8):
        dma('sp', X[:, k, :], xT[k * 128:(k + 1) * 128, :], (), ('X',), f'io{k % 3}')
    dma('pool', CB[:], consts, (), ('CB',), 'io1')
    dma('pool', MASK[:], masks, (), ('MASK',), 'io2')
    dma('sp', NG[:], ngT, (), ('NG',), 'io3')
    dma('sp', BADA[:], b_adaT, (), ('BADA',), 'io4')
    dma('sp', BG[:], b_gT, (), ('BG',), 'io5')
    dma('sp', CG[:], c_gT, (), ('CG',), 'io3')
    dma('sp', ESINK[:], a_sinkb, (), ('ESINK',), 'io4')
    dma('sp', CV[:], cvec, (), ('CV',), 'io5')
    act(ESINK[:], ESINK[:], AF.Exp, ('ESINK',), ('ESINK',))
    act(SILU[:], CV[:], AF.Silu, ('CV',), ('SILU',))
    P.barrier()

    AW_OFF = 28672

    def ada_dma(l, n):
        W = arena(AW_OFF + (n % 2) * 2048, 2048).rearrange("p (k n) -> p k n", k=8)
        wload(W, w_ada[l][:, n * 256:(n + 1) * 256], ('AW', n % 2))

    def ada_mm(l, n):
        W = arena(AW_OFF + (n % 2) * 2048, 2048).rearrange("p (k n) -> p k n", k=8)
        pm = ps[7]
        for mm_ in range(2):
            m = n * 2 + mm_
            for k in range(8):
                mm(pm[:, 2 * m:2 * m + 2], W[:, k, mm_ * 128:(mm_ + 1) * 128], SILU[:, k, :], k == 0, k == 7,
                   (('AW', n % 2), 'SILU'), (('ps', id(pm)),))

    def ada_finish(l):
        MODS = MODSL[l % 2]
        mt = ('MODS', l % 2)
        pm = ps[7]
        tt('dve', MODS[:, :, 0:48].rearrange("p j m -> p m j"),
           pm[:, 0:96].rearrange("p (m j) -> p m j", j=2),
           BADA[:, l, :].unsqueeze(2).to_broadcast([128, 48, 2]), ALU.add, (('ps', id(pm)), 'BADA'), (mt,))
        for which in range(2):
            stt('dve', MODS[:, :, 48 + 8 * which:56 + 8 * which], MODS[:, :, 8 + 24 * which:16 + 24 * which], 1.0,
                NG[:, 2 * l + which, :].unsqueeze(1).to_broadcast([128, 2, 8]), ALU.add, ALU.mult,
                (mt, 'NG'), (mt,))

    def ada_mods(l):
        for n in range(24):
            ada_dma(l, n)
            ada_mm(l, n)
        ada_finish(l)

    def rms_stats(src_fn, nchunks, ncols, inv_n, lhsT, tagsrc, bank):
        pss = bank
        for k in range(nchunks):
            sq = SQB[:, k % 2, 0:ncols]
            act(sq, src_fn(k), AF.Square, tagsrc(k), (('SQB', k % 2),))
            mm(pss[:, 0:ncols], lhsT, sq, k == 0, k == nchunks - 1, (('SQB', k % 2), 'CB', 'ONESB'), (('ps', id(bank)),))
        r = SCR[:, 0, 0:ncols]
        act(r, pss[:, 0:ncols], AF.Sqrt, (('ps', id(bank)), 'EPSC'), (('SCR', 0),), bias=EPSC[:, 0:1], scale=inv_n)
        recip(r, r, (('SCR', 0),), (('SCR', 0),))
        return r

    def norm_tile(t, which, l, out_final=None):
        MODS = MODSL[l % 2]
        j = 0 if t == 4 else 1
        cols = tsl(t)
        r = rms_stats(lambda k: X[:, k, cols], 8, 512, 1.0 / D, ONESB[:], lambda k: (('X', t),), ps[6])
        for k in range(8):
            tmp = scr(1 + k % 2)
            tt('dve', tmp, X[:, k, cols], r, ALU.mult, (('X', t), ('SCR', 0)), (('SCR', 1 + k % 2),))
            if out_final is None:
                act(H[:, k, cols], tmp, AF.Identity, (('SCR', 1 + k % 2), ('MODS', l % 2)), (('H', t),),
                    bias=MODS[:, j, 24 * which + k:24 * which + k + 1], scale=MODS[:, j, 48 + 8 * which + k:49 + 8 * which + k])
            else:
                o = scr(3)
                act(o, tmp, AF.Identity, (('SCR', 1 + k % 2), 'NG'), (('SCR', 3),), scale=NG[:, 2 * DEPTH, k:k + 1])
                dma('sp', yT[k * 128:(k + 1) * 128, cols], o, (('SCR', 3),), (), io_stream())

    def norm_phase(which, l):
        MODS = MODSL[l % 2]
        mtag = ('MODS', l % 2)

        def bank_of(t):
            return ps[6] if t % 2 == 0 else ps[7]

        def rslot(t):
            return 0 if t % 2 == 0 else 3

        def stats_step(t, k):
            cols = tsl(t)
            b = bank_of(t)
            sq = SQB[:, k % 2, :]
            act(sq, X[:, k, cols], AF.Square, (('X', t),), (('SQB', k % 2),))
            mm(b[:], ONESB[:], sq, k == 0, k == 7, (('SQB', k % 2), 'ONESB'), (('ps', id(b)),))

        def stats_fin(t):
            b = bank_of(t)
            r = scr(rslot(t))
            act(r, b[:], AF.Sqrt, (('ps', id(b)), 'EPSC'), (('SCR', rslot(t)),), bias=EPSC[:, 0:1], scale=1.0 / D)
            recip(r, r, (('SCR', rslot(t)),), (('SCR', rslot(t)),))

        def mod_step(t, k):
            j = 0 if t == 4 else 1
            cols = tsl(t)
            tmp = scr(1 + k % 2)
            tt('dve', tmp, X[:, k, cols], scr(rslot(t)), ALU.mult, (('X', t), ('SCR', rslot(t))), (('SCR', 1 + k % 2),))
            act(H[:, k, cols], tmp, AF.Identity, (('SCR', 1 + k % 2), mtag), (('H', t),),
                bias=MODS[:, j, 24 * which + k:24 * which + k + 1], scale=MODS[:, j, 48 + 8 * which + k:49 + 8 * which + k])

        for k in range(8):
            stats_step(0, k)
        stats_fin(0)
        for t in range(NT):
            for k in range(8):
                if t + 1 < NT:
                    stats_step(t + 1, k)
                mod_step(t, k)
            if t + 1 < NT:
                stats_fin(t + 1)

    def out_proj(l, w_dram, gate_off):
        MODS = MODSL[l % 2]
        WT = [arena(i * 4096, 4096).rearrange("p (k n) -> p k n", k=8) for i in range(NW)]
        for wt in range(2):
            W = WT[wt % NW]
            tag = ('WT', wt % NW)
            wload(W, w_dram[:, wt * 512:(wt + 1) * 512], tag)
            for t in range(NT):
                j = 0 if t == 4 else 1
                for mm_ in range(4):
                    m = wt * 4 + mm_
                    pb = rotate('proj', [ps[4], ps[5]])
                    for k in range(8):
                        mm(pb[:], W[:, k, mm_ * 128:(mm_ + 1) * 128], H[:, k, tsl(t)], k == 0, k == 7,
                           (tag, ('H', t)), (('ps', id(pb)),))
                    stt('dve', X[:, m, tsl(t)], pb[:], MODS[:, j, gate_off + m:gate_off + m + 1], X[:, m, tsl(t)],
                        ALU.mult, ALU.add, (('ps', id(pb)), ('MODS', l % 2), ('X', t)), (('X', t),))

    def mlp(l):
        MODS = MODSL[l % 2]
        NWP = 3
        WI = [arena(i * 8192, 4096).rearrange("p (k n) -> p k n", k=8) for i in range(NWP)]
        WO = [arena(i * 8192 + 4096, 4096).rearrange("p (k n) -> p k n", k=4) for i in range(NWP)]
        U = [arena(NWP * 8192 + i * 2048, 2048).rearrange("p (k n) -> p k n", k=4) for i in range(2)]

        def load(pc):
            wload(WI[pc % NWP], w_mlp_in[l][:, pc * 512:(pc + 1) * 512], ('WI', pc % NWP))
            wload(WO[pc % NWP], w_mlp_out[l][pc * 512:(pc + 1) * 512, :], ('WO', pc % NWP))

        def p1(pc, t, ui):
            wi, ti, u = WI[pc % NWP], ('WI', pc % NWP), U[ui]
            for jj in range(4):
                pb = rotate('mlp1', [ps[0], ps[1]])
                for k in range(8):
                    mm(pb[:], wi[:, k, jj * 128:(jj + 1) * 128], H[:, k, tsl(t)], k == 0, k == 7,
                       (ti, ('H', t)), (('ps', id(pb)),))
                si = rotate('mlpscr', [1, 2])
                act(scr(si), pb[:], AF.Relu, (('ps', id(pb)),), (('SCR', si),))
                tt('dve', u[:, jj, :], scr(si), scr(si), ALU.mult, (('SCR', si),), (('U', ui, jj),))

        def p2(pc, t, ui):
            wo, to, u = WO[pc % NWP], ('WO', pc % NWP), U[ui]
            j = 0 if t == 4 else 1
            for m in range(8):
                pb = rotate('mlp2', [ps[2], ps[3]])
                for jj in range(4):
                    mm(pb[:], wo[:, jj, m * 128:(m + 1) * 128], u[:, jj, :], jj == 0, jj == 3,
                       (to, ('U', ui, jj)), (('ps', id(pb)),))
                stt('dve', X[:, m, tsl(t)], pb[:], MODS[:, j, 40 + m:41 + m], X[:, m, tsl(t)],
                    ALU.mult, ALU.add, (('ps', id(pb)), ('MODS', l % 2), ('X', t)), (('X', t),))

        steps = [(pc, t) for pc in range(8) for t in range(NT)]
        load(0)
        load(1)
        for i, (pc, t) in enumerate(steps):
            if t == 0 and 1 <= pc < 7:
                load(pc + 1)
            p1(pc, t, i % 2)
            if i > 0:
                p2(steps[i - 1][0], steps[i - 1][1], (i - 1) % 2)
            if l + 1 < nlayers:
                if 1 <= i <= 24:
                    ada_mm(l + 1, i - 1)
                if i < 24:
                    ada_dma(l + 1, i)
        p2(steps[-1][0], steps[-1][1], (len(steps) - 1) % 2)
        if l + 1 < nlayers:
            ada_finish(l + 1)

    PT = [None] * 3
    ONB = [None]

    def attn_job(kts, rhs, N, nsub, scale, sink, dest, u, qtags, PTb, ONBb, htag):
        pO = rotate('pO', [ps[3], ps[4]])
        otag = ('ps', id(pO))
        sub = N // nsub
        nkt = len(kts)
        LOOK = 2
        banks = {}

        def qk(i):
            lhsT, vaug, mask, ktags = kts[i]
            pS = rotate('pS', [ps[0], ps[1], ps[2]])
            banks[i] = pS
            so = pS[:, 0:N] if len(rhs.shape) == 2 else pS[:, 0:N].rearrange("p (h q) -> p h q", q=128)
            mm(so, lhsT, rhs, True, True, tuple(ktags) + tuple(qtags), (('ps', id(pS)),))
        for i in range(min(LOOK, nkt)):
            qk(i)
        for i, (lhsT, vaug, mask, ktags) in enumerate(kts):
            if i == min(2, nkt - 1):
                attn_flush()
            if i + LOOK < nkt:
                qk(i + LOOK)
            pS = banks.pop(i)
            stag = ('ps', id(pS))
            pi = rot.get('PT', 0) % 3
            rot['PT'] = rot.get('PT', 0) + 1
            pt = PTb[pi]
            ptag = ('PT', pi)
            act(pt[:, 0:N], pS[:, 0:N], AF.Exp, (stag,), (ptag,), scale=scale)
            if mask is not None:
                tt('dve', pt[:, 0:N].rearrange("p (h q) -> p h q", q=128), pt[:, 0:N].rearrange("p (h q) -> p h q", q=128),
                   mask.unsqueeze(1).to_broadcast([128, N // 128, 128]), ALU.mult, (ptag, 'MASK'), (ptag,))
            for s in range(nsub):
                if nsub == 1:
                    mm(pO[0:65, 0:N], vaug, pt[:, 0:N], i == 0, i == len(kts) - 1, (ptag,) + tuple(ktags), (otag,))
                else:
                    mm(pO[0:65, s * sub:(s + 1) * sub], vaug, pt[:, s * sub:(s + 1) * sub], i == 0 and s == 0, False,
                       (ptag,) + tuple(ktags), (otag,), skip=True)
        pending.append(lambda: attn_epilogue(pO, otag, N, sink, dest, u, ONBb, htag))

    pending = []

    def attn_flush():
        lists = [p() if callable(p) else p for p in pending]
        del pending[:]
        while any(lists):
            for L in lists:
                if L:
                    L.pop(0)()

    def attn_epilogue(pO, otag, N, sink, dest, u, ONBb, htag, ebanks=None, slots=(0, 1)):
        ebanks = ebanks or [ps[5], ps[6]]
        ds, os_ = slots
        den = SCR[64:65, ds, 0:N]
        dtag_, ostag = ('SCR', ds), ('SCR', os_)
        osb = SCR[0:64, os_, 0:N]
        three = len(dest.shape) != 2
        st = []
        box = {}

        def e0():
            if sink is not None:
                tt('dve', den.rearrange("p (h q) -> p h q", q=128), pO[64:65, 0:N].rearrange("p (h q) -> p h q", q=128),
                   sink.unsqueeze(2).to_broadcast([1, N // 128, 128]), ALU.add, (otag, 'ESINK'), (dtag_,))
                recip(den, den, (dtag_,), (dtag_,))
            else:
                recip(den, pO[64:65, 0:N], (otag,), (dtag_,))
        st.append(e0)

        def e1():
            pB = rotate('pB', ebanks)
            box['pB'] = pB
            mm(pB[0:64, 0:N], ONESF[64:65, 0:64], den, True, True, (dtag_, 'ONESF'), (('ps', id(pB)),))
            cp('act', osb, pO[0:64, 0:N], (otag,), (ostag,))
        st.append(e1)

        def e2():
            pB = box['pB']
            btag = ('ps', id(pB))
            if u == 0:
                tt('dve', dest, osb.rearrange("p (h q) -> p h q", q=128) if three else osb,
                   pB[0:64, 0:N].rearrange("p (h q) -> p h q", q=128) if three else pB[0:64, 0:N],
                   ALU.mult, (ostag, btag), (htag,))
            else:
                tt('dve', ONBb[0:64, 0:N], osb, pB[0:64, 0:N], ALU.mult, (ostag, btag), ('ONB',))
        st.append(e2)
        if u == 1:
            def e3():
                pM = rotate('pB', ebanks)
                box['pM'] = pM
                mm(pM[64:128, 0:N], IDENT[0:64, 0:64], ONBb[0:64, 0:N], True, True, ('ONB', 'CB'), (('ps', id(pM)),))
            st.append(e3)

            def e4():
                pM = box['pM']
                cp('act', dest, pM[64:128, 0:N].rearrange("p (h q) -> p h q", q=128) if three else pM[64:128, 0:N],
                   (('ps', id(pM)),), (htag,))
            st.append(e4)
        return st

    def attn_pair(jobs, N, nsub, scale, qtags, PTb, ONBb, side=()):
        nkt = len(jobs[0]['kts'])
        sub = N // nsub
        PT4 = [PTb[0], PTb[1], PTb[2], SQB[:, 1, :]]
        PT4t = [('PT', 0), ('PT', 1), ('PT', 2), ('SQB', 1)]
        pSb = [[ps[0], ps[1]], [ps[2], ps[3]]]
        pOb = [ps[4], ps[5]]

        def qk(x, i):
            jb = jobs[x]
            lhsT, vaug, mask, ktags = jb['kts'][i]
            pS = pSb[x][i % 2]
            rhs = jb['rhs']
            so = pS[:, 0:N] if len(rhs.shape) == 2 else pS[:, 0:N].rearrange("p (h q) -> p h q", q=128)
            mm(so, lhsT, rhs, True, mask is None, tuple(ktags) + tuple(qtags), (('ps', id(pS)),))

        def qmask(x, i):
            lhsT, vaug, mask, ktags = jobs[x]['kts'][i]
            if mask is not None:
                pS = pSb[x][i % 2]
                mm(pS[:, 0:N], IDENT, mask, False, True, ('MNEG', 'CB'), (('ps', id(pS)),))
        qk(0, 0)
        qk(1, 0)
        qmask(0, 0)
        qmask(1, 0)
        for i in range(nkt):
            if i + 1 < nkt:
                qk(0, i + 1)
                qk(1, i + 1)
                qmask(0, i + 1)
                qmask(1, i + 1)
            for x in range(2):
                lhsT, vaug, mask, ktags = jobs[x]['kts'][i]
                pS = pSb[x][i % 2]
                pt, ptag = PT4[2 * x + i % 2], PT4t[2 * x + i % 2]
                act(pt[:, 0:N], pS[:, 0:N], AF.Exp, (('ps', id(pS)),), (ptag,), scale=scale)
            if i == 0 and nkt > 1:
                continue
            if i == min(1, nkt - 1):
                attn_flush()
            for ii in ([0, 1] if (i == 1 and nkt > 1) else [i]):
                for x in range(2):
                    lhsT, vaug, mask, ktags = jobs[x]['kts'][ii]
                    pt, ptag = PT4[2 * x + ii % 2], PT4t[2 * x + ii % 2]
                    pO = pOb[x]
                    otag = ('ps', id(pO))
                    for s_ in range(nsub):
                        if nsub == 1:
                            mm(pO[0:65, 0:N], vaug, pt[:, 0:N], ii == 0, ii == nkt - 1, (ptag,) + tuple(ktags), (otag,))
                        else:
                            mm(pO[0:65, s_ * sub:(s_ + 1) * sub], vaug, pt[:, s_ * sub:(s_ + 1) * sub], ii == 0 and s_ == 0, False,
                               (ptag,) + tuple(ktags), (otag,), skip=True)
            if side and i >= 1:
                side.pop(0)()
        for x in range(2):
            jb = jobs[x]
            pending.append(lambda pO=pOb[x], jb=jb, x=x: attn_epilogue(pO, ('ps', id(pO)), N, jb['sink'], jb['dest'], jb['u'], ONBb, jb['htag'],
                                                                   ebanks=[ps[6], ps[7]], slots=((0, 1) if x == 0 else (3, 2))))

    def rope_apply(src_ps, rows, ncols, cos, sin, perm, dst, srctag, dsttags, p2bank=None):
        t1 = SCR[rows, 1, 0:ncols]
        t2 = SCR[rows, 2, 0:ncols]
        kb = SQB[rows, 0, 0:ncols]
        if _RL < 1:
            return
        tt('dve', t1, src_ps, cos, ALU.mult, (srctag, 'ROPE'), (('SCR', 1),))
        if _RL < 2:
            return
        cp('act', kb, src_ps, (srctag,), (('SQB', 0),))
        if _RL < 3:
            return
        p2 = p2bank if p2bank is not None else rotate('pB', [ps[5], ps[6]])
        mm(p2[rows, 0:ncols], perm, kb, True, True, (('SQB', 0), 'CB'), (('ps', id(p2)),))
        if _RL < 4:
            return
        tt('dve', t2, p2[rows, 0:ncols], sin, ALU.mult, (('ps', id(p2)), 'ROPE'), (('SCR', 2),))
        tt('dve', dst, t1, t2, ALU.add, (('SCR', 1), ('SCR', 2)), dsttags)

    def gqa_layer(l, kind, j, w_qkv, w_o, ctx_k, ctx_v, st_k, st_v):
        isA = kind == 'A'
        if isA:
            NK, NKT = 3072, 24
            K_HL, K_HR, K_CTX, K_P = 2048, 2176, 2304, 2560
        else:
            NK, NKT = 4864, 38
            K_CTX, K_P = 4096, 4352
        off = 0
        KB = arena(off, 2 * NK).rearrange("p (c n) -> p c n", c=2); off += 2 * NK
        VB = arena(off, NKT * 264).rearrange("p (t n) -> p t n", n=264); off += NKT * 264
        WT = [arena(off, 4096).rearrange("p (k n) -> p k n", k=8)]; off += 4096
        QTB = [arena(off + i * 4096, 4096).rearrange("p (c n) -> p c n", c=8) for i in range(2)]; off += 8192
        PTb = [arena(off + i * 512, 512) for i in range(3)]; off += 1536
        ONBb = arena(off, 512); off += 512
        VB4 = VB.rearrange("p t (g e) -> p t g e", e=66)
        if isA:
            MNEG = arena(off, 2048).rearrange("p (m n) -> p m n", m=4); off += 2048
            for m_ in range(4):
                ts('dve', MNEG[:, m_, :].rearrange("p (h q) -> p h q", q=128), MASK[:, m_, :].unsqueeze(1).to_broadcast([128, 4, 128]),
                   -1.0, 30000.0, ALU.add, ALU.mult, ('MASK',), ('MNEG',))
        memset('dve', VB[:], 1.0, ('VB',))
        W = WT[0]
        wload(W, w_qkv[:, 1024:1536], ('WT', 0))
        dma('pool', KB[:, :, K_CTX:K_CTX + 256], ctx_k.rearrange("(c p) t -> p c t", p=128), (), ('KB',), 'io0')
        for s_ in range(2):
            dma('pool', VB4[:, K_CTX // 128 + s_, :, 0:64],
                ctx_v[s_ * 128:(s_ + 1) * 128, :].rearrange("p (g d) -> p g d", d=64), ('VB',), ('VB',), 'io1')
        if _SUB == 11:
            return
        for t in range(NT):
            if _SUB in (12, 13) and t > 0:
                return
            if _SUB == 14 and t < 4:
                continue
            samp = t < 4
            cols = tsl(t)
            if samp:
                dma('sp', ROPE[:, :, :], rope[0:2, :, cols].rearrange("a p n -> p a n"), (), ('ROPE',), 'io2')
            kcol0 = t * 512 if samp else K_P
            for c in range(2):
                pb = rotate('proj', [ps[6], ps[7]])
                ptag = ('ps', id(pb))
                for k in range(8):
                    mm(pb[:], W[:, k, c * 128:(c + 1) * 128], H[:, k, cols], k == 0, k == 7, (('WT', 0), ('H', t)), (ptag,))
                src = pb[:]
                srctag = ptag
                if not isA:
                    sq = SQB[:, 1, :]
                    act(sq, pb[:], AF.Square, (ptag,), (('SQB', 1),))
                    pn = rotate('pB', [ps[5], ps[6]]) if False else ps[4]
                    mm(pn[:], BDIAG, sq, True, True, (('SQB', 1), 'CB'), (('ps', id(pn)),))
                    r = scr(0)
                    act(r, pn[:], AF.Sqrt, (('ps', id(pn)), 'EPSC'), (('SCR', 0),), bias=EPSC[:, 0:1], scale=1.0 / 64)
                    recip(r, r, (('SCR', 0),), (('SCR', 0),))
                    kn = scr(3)
                    stt('dve', kn, pb[:], CG[:, 1:2], r, ALU.mult, ALU.mult, (ptag, 'CG', ('SCR', 0)), (('SCR', 3),))
                    src = kn
                    srctag = ('SCR', 3)
                dstK = KB[:, c, kcol0:kcol0 + 512]
                if samp:
                    rope_apply(src, slice(0, 128), 512, ROPE[:, 0, :], ROPE[:, 1, :], PERM64, dstK, srctag, ('KB',))
                else:
                    if isA:
                        kf = scr(3)
                        cp('dve', kf, pb[:], (ptag,), (('SCR', 3),))
                    else:
                        kf = src
                    cp('act', dstK, kf, (('SCR', 3),), ('KB',))
                    dma('sp', st_k[c * 128:(c + 1) * 128, :], kf, (('SCR', 3),), (), io_stream())
            for s in range(4):
                if _SUB == 12:
                    break
                tok = slice(t * 512 + s * 128, t * 512 + (s + 1) * 128)
                pb = rotate('proj', [ps[6], ps[7]])
                ptag = ('ps', id(pb))
                for k in range(8):
                    mm(pb[:, 0:256], H[:, k, tok], W[:, k, 256:512], k == 0, k == 7, (('WT', 0), ('H', t)), (ptag,))
                kt = (kcol0 // 128) + s
                cp('act', VB4[:, kt, :, 0:64], pb[:, 0:256].rearrange("p (g d) -> p g d", d=64), (ptag,), ('VB',))
                if not samp:
                    vf = scr(2)
                    cp('dve', vf[:, 0:256], pb[:, 0:256], (ptag,), (('SCR', 2),))
                    dma('sp', st_v[s * 128:(s + 1) * 128, :], vf[:, 0:256], (('SCR', 2),), (), io_stream())
        if _SUB == 1:
            return
        if isA:
            ki, ko, vi, vo = exAkI[j], exAkO[j], exAvI[j], exAvO[j]
            for c in range(2):
                dma('sp', ki[c * 128:(c + 1) * 128, 0:128], KB[:, c, 0:128], ('KB',), ('exki',), 'ex0')
                dma('sp', ki[c * 128:(c + 1) * 128, 128:256], KB[:, c, 1920:2048], ('KB',), ('exki',), 'ex1')
            dma('sp', vi[0:128, :].rearrange("p (g d) -> p g d", d=64), VB4[:, 0, :, 0:64], ('VB',), ('exvi',), 'ex2')
            dma('sp', vi[128:256, :].rearrange("p (g d) -> p g d", d=64), VB4[:, 15, :, 0:64], ('VB',), ('exvi',), 'ex3')
        else:
            ki, ko, vi, vo = exK_in['C'], exK_out['C'], exV_in, exV_out
            for c in range(2):
                dma('sp', ki[c * 128:(c + 1) * 128, :], KB[:, c, 0:TS], ('KB',), ('exki',), f'ex{c}')
            for s_ in range(16):
                dma('sp', vi[s_ * 128:(s_ + 1) * 128, :].rearrange("p (g d) -> p g d", d=64), VB4[:, s_, :, 0:64], ('VB',), ('exvi',), f'ex{2 + s_ % 2}')
        P.add('pool', lambda e, a=ki, b=ko: e.collective_compute("AllGather", ALU.bypass, replica_groups=[[0, 1], [2, 3], [4, 5], [6, 7]],
                                                                 ins=[a.ap().opt()], outs=[b.ap().opt()]),
              ('exki',), ('exko',), kind='cc', stream=f'cc{l}k')
        P.add('pool', lambda e, a=vi, b=vo: e.collective_compute("AllGather", ALU.bypass, replica_groups=[[0, 1], [2, 3], [4, 5], [6, 7]],
                                                                 ins=[a.ap().opt()], outs=[b.ap().opt()]),
              ('exvi',), ('exvo',), kind='cc', stream=f'cc{l}v')
        if isA:
            for c in range(2):
                dma('sp', KB[:, c, K_HL:K_HL + 128], ko[c * 128:(c + 1) * 128, 128:256], ('exko',), ('KB',), 'ex0')
                dma('sp', KB[:, c, K_HR:K_HR + 128], ko[256 + c * 128:256 + (c + 1) * 128, 0:128], ('exko',), ('KB',), 'ex1')
            dma('sp', VB4[:, 16, :, 0:64], vo[128:256, :].rearrange("p (g d) -> p g d", d=64), ('exvo',), ('VB',), 'ex2')
            dma('sp', VB4[:, 17, :, 0:64], vo[256:384, :].rearrange("p (g d) -> p g d", d=64), ('exvo',), ('VB',), 'ex3')
        else:
            for r in range(2):
                for c in range(2):
                    dma('sp', KB[:, c, r * TS:(r + 1) * TS], ko[r * 256 + c * 128:r * 256 + (c + 1) * 128, :], ('exko',), ('KB',), f'ex{c}')
                for s_ in range(16):
                    dma('sp', VB4[:, r * 16 + s_, :, 0:64],
                        vo[r * TS + s_ * 128:r * TS + (s_ + 1) * 128, :].rearrange("p (g d) -> p g d", d=64), ('exvo',), ('VB',), f'ex{2 + s_ % 2}')
        if _SUB == 2:
            return
        scale = 0.125

        def qproj_stages(t, c):
            samp = t < 4
            cols = tsl(t)
            QTn = QTB[t % 2]
            Wq, wtag = WT[0], ('WT', 0)
            mm_ = c % 4
            pb = ps[6]
            ptag = ('ps', id(pb))
            dtag = (('QT', t % 2, c),)
            st = []

            def s_proj():
                if c == 0 and samp:
                    dma('sp', ROPE[:, :, :], rope[0:2, :, cols].rearrange("a p n -> p a n"), (), ('ROPE',), 'io2')
                for k in range(8):
                    mm(pb[:], Wq[:, k, mm_ * 128:(mm_ + 1) * 128], H[:, k, cols], k == 0, k == 7, (wtag, ('H', t)), (ptag,))
            st.append(s_proj)
            src, srctag = pb[:], ptag
            if not isA:
                sq = SQB[:, 0, :]
                pn = ps[7]
                r = scr(0)
                qn = scr(3)
                st.append(lambda: act(sq, pb[:], AF.Square, (ptag,), (('SQB', 0),)))
                st.append(lambda: mm(pn[:], BDIAG, sq, True, True, (('SQB', 0), 'CB'), (('ps', id(pn)),)))
                st.append(lambda: act(r, pn[:], AF.Sqrt, (('ps', id(pn)), 'EPSC'), (('SCR', 0),), bias=EPSC[:, 0:1], scale=1.0 / 64))

                def s_qn():
                    recip(r, r, (('SCR', 0),), (('SCR', 0),))
                    stt('dve', qn, pb[:], CG[:, 0:1], r, ALU.mult, ALU.mult, (ptag, 'CG', ('SCR', 0)), (('SCR', 3),))
                st.append(s_qn)
                src, srctag = qn, ('SCR', 3)
            if samp:
                t1, t2, kb, p2 = scr(1), scr(2), SQB[:, 0, :], ps[7]

                def s_r1():
                    tt('dve', t1, src, ROPE[:, 0, :], ALU.mult, (srctag, 'ROPE'), (('SCR', 1),))
                    cp('act', kb, src, (srctag,), (('SQB', 0),))
                st.append(s_r1)
                st.append(lambda: mm(p2[:], PERM64, kb, True, True, (('SQB', 0), 'CB'), (('ps', id(p2)),)))

                def s_r3():
                    tt('dve', t2, p2[:], ROPE[:, 1, :], ALU.mult, (('ps', id(p2)), 'ROPE'), (('SCR', 2),))
                    tt('dve', QTn[:, c, :], t1, t2, ALU.add, (('SCR', 1), ('SCR', 2)), dtag)
                st.append(s_r3)
            else:
                st.append(lambda: cp('act', QTn[:, c, :], src, (srctag,), dtag))
            st.append(lambda: qload(t, c))
            return st

        def qproj_chunk(t, c):
            for f_ in qproj_stages(t, c):
                f_()

        def qload(t, c):
            if c == 3:
                wload(WT[0], w_qkv[:, 512:1024], ('WT', 0))
            elif c == 7 and t + 1 < NT:
                wload(WT[0], w_qkv[:, 0:512], ('WT', 0))

        wload(WT[0], w_qkv[:, 0:512], ('WT', 0))
        for c in range(8):
            qproj_chunk(0, c)
        for t in range(NT):
            samp = t < 4
            cols = tsl(t)
            QT = QTB[t % 2]
            pairs = []
            if _SUB == 3:
                return
            qtag = tuple(('QT', t % 2, c) for c in range(8))
            ht = ('H', t)
            if isA:
                for gp in range(2):
                    cb = 4 * gp
                    if samp:
                        qlist = [(t * 4 + qb, qb * 128, None) for qb in range(4)]
                    else:
                        qlist = [(None, sq_ * 256 + qb * 128, sq_) for sq_ in range(2) for qb in range(2)]
                    for b, qc, sq_ in qlist:
                        jobs = []
                        for u in range(2):
                            g = 2 * gp + u
                            rows = slice(u * 64, u * 64 + 64)
                            vsl = slice(g * 66, g * 66 + 65)
                            kl = []
                            if samp:
                                for cxt in range(2):
                                    kc = K_CTX + cxt * 128
                                    kl.append((KB[rows, g // 2, kc:kc + 128], VB[:, kc // 128, vsl], None, ('KB', 'VB')))
                                if b == 0:
                                    kl.append((KB[rows, g // 2, K_HL:K_HL + 128], VB[:, K_HL // 128, vsl], MNEG[:, 2, :], ('KB', 'VB')))
                                else:
                                    kl.append((KB[rows, g // 2, (b - 1) * 128:b * 128], VB[:, b - 1, vsl], MNEG[:, 0, :], ('KB', 'VB')))
                                kl.append((KB[rows, g // 2, b * 128:(b + 1) * 128], VB[:, b, vsl], None, ('KB', 'VB')))
                                if b == 15:
                                    kl.append((KB[rows, g // 2, K_HR:K_HR + 128], VB[:, K_HR // 128, vsl], MNEG[:, 3, :], ('KB', 'VB')))
                                else:
                                    kl.append((KB[rows, g // 2, (b + 1) * 128:(b + 2) * 128], VB[:, b + 1, vsl], MNEG[:, 1, :], ('KB', 'VB')))
                            else:
                                for kk in range(2):
                                    kc = K_P + sq_ * 256 + kk * 128
                                    kl.append((KB[rows, g // 2, kc:kc + 128], VB[:, kc // 128, vsl], None, ('KB', 'VB')))
                            jobs.append(dict(kts=kl, rhs=QT[rows, cb:cb + 4, qc:qc + 128],
                                             sink=ESINK[64:65, j, g * 4:(g + 1) * 4],
                                             dest=H[rows, cb:cb + 4, t * 512 + qc:t * 512 + qc + 128], u=u, htag=ht))
                        pairs.append((jobs, 512, 1))
            else:
                for c in range(8):
                    if samp:
                        qlist = [(0, 512, None)]
                    else:
                        qlist = [(sq_ * 256, 256, sq_) for sq_ in range(2)]
                    for qc, qn, sq_ in qlist:
                        jobs = []
                        for u in range(2):
                            g = 2 * (c // 4) + u
                            rows = slice(u * 64, u * 64 + 64)
                            vsl = slice(g * 66, g * 66 + 65)
                            if samp:
                                kl = [(KB[rows, g // 2, kt * 128:(kt + 1) * 128], VB[:, kt, vsl], None, ('KB', 'VB')) for kt in range(34)]
                            else:
                                kl = []
                                for kk in range(2):
                                    kc = K_P + sq_ * 256 + kk * 128
                                    kl.append((KB[rows, g // 2, kc:kc + 128], VB[:, kc // 128, vsl], None, ('KB', 'VB')))
                            jobs.append(dict(kts=kl, rhs=QT[rows, c, qc:qc + qn], sink=None,
                                             dest=H[rows, c, t * 512 + qc:t * 512 + qc + qn], u=u, htag=ht))
                        pairs.append((jobs, qn, 1))
            for pi, (jobs, n_, ns_) in enumerate(pairs):
                side = qproj_stages(t + 1, pi) if (t + 1 < NT and pi < 8) else []
                attn_pair(jobs, n_, ns_, scale, qtag, PTb, ONBb, side)
                while side:
                    side.pop(0)()
            if t + 1 < NT:
                for c in range(len(pairs), 8):
                    qproj_chunk(t + 1, c)
            attn_flush()
        if _SUB == 4:
            attn_flush()
            return
        attn_flush()
        P.barrier()
        out_proj(l, w_o, 16)

    def mla_layer(l):
        NK, NKT = 4864, 38
        K_CTX, K_P = 4096, 4352
        off = 0
        CKV = arena(off, 2 * NK).rearrange("p (c n) -> p c n", c=2); off += 2 * NK
        KH = arena(off, NK); off += NK
        QL = arena(off, 3 * T).rearrange("p (c n) -> p c n", c=3); off += 3 * T
        VH = arena(off, NKT * 65).rearrange("p (t n) -> p t n", n=65); off += NKT * 65
        QH = arena(off, T); off += T
        WS = [arena(off + i * 544, 544) for i in range(2)]; off += 1088
        PTb = [arena(off + i * 512, 512) for i in range(3)]; off += 1536
        ONBb = arena(off, 512); off += 512
        WD = arena(2 * NK + NK + 3 * T, 8 * 672).rearrange("p (k n) -> p k n", k=8)
        scale = 96.0 ** -0.5
        wload(WD[:, :, 0:384], b_w_dq, 'WD')
        wload(WD[:, :, 384:672], b_w_dkv, 'WD')
        dma('pool', CKV[:, :, K_CTX:K_CTX + 256], ctx_b_ckv.rearrange("(c p) t -> p c t", p=128), (), ('CKV',), 'io0')
        dma('pool', KH[64:96, K_CTX:K_CTX + 256], ctx_b_kr, (), ('KH',), 'io1')
        for t in range(NT):
            samp = t < 4
            cols = tsl(t)
            kcol0 = t * 512 if samp else K_P
            if samp:
                dma('sp', ROPE[:, :, :], rope[2:4, :, cols].rearrange("a p n -> p a n"), (), ('ROPE',), 'io2')
            pbs = [ps[0], ps[1], ps[2]]
            for c in range(3):
                for k in range(8):
                    mm(pbs[c][:], WD[:, k, c * 128:(c + 1) * 128], H[:, k, cols], k == 0, k == 7, ('WD', ('H', t)), (('ps', id(pbs[c])),))
            r = rms_stats(lambda k: pbs[k][:], 3, 512, 1.0 / 384, ONESB[:], lambda k: (('ps', id(pbs[k])),), ps[3])
            for c in range(3):
                stt('dve', QL[:, c, cols], pbs[c][:], BG[:, c:c + 1], r, ALU.mult, ALU.mult,
                    (('ps', id(pbs[c])), 'BG', ('SCR', 0)), ('QL',))
            pbs2 = [ps[4], ps[5]]
            for c in range(2):
                for k in range(8):
                    mm(pbs2[c][:], WD[:, k, 384 + c * 128:384 + (c + 1) * 128], H[:, k, cols], k == 0, k == 7, ('WD', ('H', t)), (('ps', id(pbs2[c])),))
            r = rms_stats(lambda k: pbs2[k][:], 2, 512, 1.0 / 256, ONESB[:], lambda k: (('ps', id(pbs2[k])),), ps[3])
            for c in range(2):
                if samp:
                    stt('dve', CKV[:, c, kcol0:kcol0 + 512], pbs2[c][:], BG[:, 3 + c:4 + c], r, ALU.mult, ALU.mult,
                        (('ps', id(pbs2[c])), 'BG', ('SCR', 0)), ('CKV',))
                else:
                    cf = scr(3)
                    stt('dve', cf, pbs2[c][:], BG[:, 3 + c:4 + c], r, ALU.mult, ALU.mult,
                        (('ps', id(pbs2[c])), 'BG', ('SCR', 0)), (('SCR', 3),))
                    cp('act', CKV[:, c, kcol0:kcol0 + 512], cf, (('SCR', 3),), ('CKV',))
                    dma('sp', st_b_ckv[c * 128:(c + 1) * 128, :], cf, (('SCR', 3),), (), io_stream())
            pk = ps[7]
            for k in range(8):
                mm(pk[64:96, :], WD[:, k, 640:672], H[:, k, cols], k == 0, k == 7, ('WD', ('H', t)), (('ps', id(pk)),))
            if samp:
                rope_apply(pk[64:96, :], slice(64, 96), 512, ROPE[64:96, 0, :], ROPE[64:96, 1, :], PERMM[64:96, 0:32],
                           KH[64:96, kcol0:kcol0 + 512], ('ps', id(pk)), ('KH',))
            else:
                kf = SCR[64:96, 3, :]
                cp('dve', kf, pk[64:96, :], (('ps', id(pk)),), (('SCR', 3),))
                cp('act', KH[64:96, kcol0:kcol0 + 512], kf, (('SCR', 3),), ('KH',))
                dma('sp', st_b_kr[:, :], kf, (('SCR', 3),), (), io_stream())
        ki, ko = exK_in['B'], exK_out['B']
        for c in range(2):
            dma('sp', ki[c * 128:(c + 1) * 128, :], CKV[:, c, 0:TS], ('CKV',), ('exki',), f'ex{c}')
        dma('sp', exR_in[:, :], KH[64:96, 0:TS], ('KH',), ('exri',), 'ex2')
        P.add('pool', lambda e, a=ki, b=ko: e.collective_compute("AllGather", ALU.bypass, replica_groups=[[0, 1], [2, 3], [4, 5], [6, 7]],
                                                                 ins=[a.ap().opt()], outs=[b.ap().opt()]),
              ('exki',), ('exko',), kind='cc', stream=f'cc{l}k')
        P.add('pool', lambda e, a=exR_in, b=exR_out: e.collective_compute("AllGather", ALU.bypass, replica_groups=[[0, 1], [2, 3], [4, 5], [6, 7]],
                                                                           ins=[a.ap().opt()], outs=[b.ap().opt()]),
              ('exri',), ('exro',), kind='cc', stream=f'cc{l}r')
        for r_ in range(2):
            for c in range(2):
                dma('sp', CKV[:, c, r_ * TS:(r_ + 1) * TS], ko[r_ * 256 + c * 128:r_ * 256 + (c + 1) * 128, :], ('exko',), ('CKV',), f'ex{c}')
            dma('sp', KH[64:96, r_ * TS:(r_ + 1) * TS], exR_out[r_ * 32:(r_ + 1) * 32, :], ('exro',), ('KH',), 'ex2')
        P.barrier()
        memset('dve', VH[:], 1.0, ('VH',))
        for h in range(16):
            ws = WS[h % 2]
            wtag = ('WS', h % 2)
            WUQ = ws[:, 0:288].rearrange("p (k n) -> p k n", k=3)
            WUKV = ws[:, 288:544].rearrange("p (k n) -> p k n", k=2)
            wload(WUQ, b_w_uq[:, h * 96:(h + 1) * 96], wtag)
            wload(WUKV, b_w_ukv[:, h * 128:(h + 1) * 128], wtag)
            for kc in range(0, NK, 512):
                n = min(512, NK - kc)
                pb = rotate('proj', [ps[6], ps[7]])
                ptag = ('ps', id(pb))
                for k in range(2):
                    mm(pb[0:64, 0:n], WUKV[:, k, 0:64], CKV[:, k, kc:kc + n], k == 0, k == 1, (wtag, 'CKV'), (ptag,))
                cp('dve', KH[0:64, kc:kc + n], pb[0:64, 0:n], (ptag,), ('KH',))
            for k0 in range(0, NKT, 8):
                nk = min(8, NKT - k0)
                pb = rotate('proj', [ps[6], ps[7]])
                ptag = ('ps', id(pb))
                for i in range(nk):
                    kt = k0 + i
                    for k in range(2):
                        mm(pb[:, i * 64:(i + 1) * 64], CKV[:, k, kt * 128:(kt + 1) * 128], WUKV[:, k, 64:128], k == 0, k == 1,
                           (wtag, 'CKV'), (ptag,))
                cp('dve', VH[:, k0:k0 + nk, 0:64], pb[:, 0:nk * 64].rearrange("p (t d) -> p t d", d=64), (ptag,), ('VH',))
            for t in range(NT):
                samp = t < 4
                cols = tsl(t)
                pb = rotate('proj', [ps[6], ps[7]])
                ptag = ('ps', id(pb))
                for k in range(3):
                    mm(pb[0:96, :], WUQ[:, k, :], QL[:, k, cols], k == 0, k == 2, (wtag, 'QL'), (ptag,))
                cp('act', QH[0:64, cols], pb[0:64, :], (ptag,), ('QH',))
                if samp:
                    dma('sp', ROPE[:, :, :], rope[2:4, :, cols].rearrange("a p n -> p a n"), (), ('ROPE',), 'io2')
                    rope_apply(pb[64:96, :], slice(64, 96), 512, ROPE[64:96, 0, :], ROPE[64:96, 1, :], PERMM[64:96, 0:32],
                               QH[64:96, cols], ptag, ('QH',))
                else:
                    cp('act', QH[64:96, cols], pb[64:96, :], (ptag,), ('QH',))
            u = h % 2
            rows = slice(u * 64, u * 64 + 64)
            for t in range(NT):
                cols = tsl(t)
                if t < 4:
                    kl = [(KH[0:96, kt * 128:(kt + 1) * 128], VH[:, kt, :], None, ('KH', 'VH')) for kt in range(34)]
                    attn_job(kl, QH[0:96, cols], 512, 1, scale, None, H[rows, h // 2, cols], u, ('QH',), PTb, ONBb, ('H', t))
                else:
                    for sq_ in range(2):
                        kl = []
                        for kk in range(2):
                            kc = K_P + sq_ * 256 + kk * 128
                            kl.append((KH[0:96, kc:kc + 128], VH[:, kc // 128, :], None, ('KH', 'VH')))
                        qc = t * 512 + sq_ * 256
                        attn_job(kl, QH[0:96, qc:qc + 256], 256, 1, scale, None, H[rows, h // 2, qc:qc + 256], u, ('QH',), PTb, ONBb, ('H', t))
        attn_flush()
        P.barrier()
        out_proj(l, b_w_o, 16)

    for l in range(nlayers):
        kind = 'ABC'[l % 3]
        j = l // 3
        last = l == nlayers - 1
        if last and stop < 2:
            break
        if l == 0:
            ada_mods(l)
            P.barrier()
        norm_phase(0, l)
        if last and stop < 3:
            break
        if kind == 'A':
            gqa_layer(l, 'A', j, a_w_qkv[j], a_w_o[j], ctx_a_k[j], ctx_a_v[j], st_a_k[j], st_a_v[j])
        elif kind == 'B':
            mla_layer(l)
        else:
            gqa_layer(l, 'C', 0, c_w_qkv, c_w_o, ctx_c_k, ctx_c_v, st_c_k, st_c_v)
        P.barrier()
        if last and stop < 4:
            break
        norm_phase(1, l)
        mlp(l)
        P.barrier()
    for t in range(NT):
        norm_tile(t, 0, 0, out_final=True)

    P.analyze()
    sems = {}
    for e in Prog.ENGS:
        n = (P.nsig[e] + CH - 1) // CH
        sems[e] = [es.enter_context(nc.semaphore(f"s_{e}{i}")) for i in range(max(n, 1))]
    ssem = {s: es.enter_context(nc.semaphore(f"d_{s}")) for s in P.stream_cnt}

    def emit(e, eng):
        for o in P.eng_ops[e]:
            for d in o.cwaits:
                k = d.sigk - 1
                eng.wait_ge(sems[d.eng][k // CH], k % CH + 1)
            for d in o.swaits:
                eng.wait_ge(ssem[d.stream], 1 if d.kind == 'cc' else 16 * d.dman)
            if o.fn is None:
                continue
            ins = o.fn(eng)
            if o.kind == 'dma':
                ins.then_inc(ssem[o.stream], 16)
            elif o.kind == 'cc':
                ins.then_inc(ssem[o.stream])
            elif o.sig:
                k = o.sigk - 1
                ins.then_inc(sems[e][k // CH], 1)
        if e == 'sp':
            for s, n in P.stream_cnt.items():
                if not s.startswith('cc'):
                    eng.wait_ge(ssem[s], 16 * n)

    with nc.Block() as block:
        @block.tensor
        def _(eng):
            emit('pe', eng)

        @block.scalar
        def _(eng):
            emit('act', eng)

        @block.vector
        def _(eng):
            emit('dve', eng)

        @block.gpsimd
        def _(eng):
            emit('pool', eng)

        @block.sync
        def _(eng):
            emit('sp', eng)
    es.close()
    return nc


def _rope_tables(hf):
    t = np.arange(TS) + hf * TS
    rows = (t // 64).astype(np.float64)
    colp = (t % 64).astype(np.float64)
    out = np.zeros((4, 128, TS), np.float32)

    def fill(cos_t, sin_t, p0, nd, pos):
        hlf = nd // 2
        fr = THETA ** (-np.arange(0, nd, 2, dtype=np.float32) / nd)
        ang = (pos.astype(np.float32)[None, :] * fr[:, None].astype(np.float32)).astype(np.float32)
        c, s = np.cos(ang), np.sin(ang)
        cos_t[p0:p0 + hlf] = c
        cos_t[p0 + hlf:p0 + nd] = c
        sin_t[p0:p0 + hlf] = -s
        sin_t[p0 + hlf:p0 + nd] = s
    for base in (0, 64):
        fill(out[0], out[1], base, 32, rows)
        fill(out[0], out[1], base + 32, 32, colp)
    fill(out[2], out[3], 64, 16, rows)
    fill(out[2], out[3], 80, 16, colp)
    return out


def _consts():
    c = np.zeros((128, 6, 128), np.float32)
    c[:, 0, :] = np.eye(128)
    for base in (0, 32, 64, 96):
        for i in range(16):
            c[base + i + 16, 1, base + i] = 1.0
            c[base + i, 1, base + i + 16] = 1.0
    for base in (0, 16):
        for i in range(8):
            c[64 + base + i + 8, 2, base + i] = 1.0
            c[64 + base + i, 2, base + i + 8] = 1.0
    c[0:64, 3, 0:64] = 1.0
    c[64:128, 3, 64:128] = 1.0
    return c


def _masks(hf):
    k = np.arange(128)[:, None]
    q = np.arange(128)[None, :]
    m = np.zeros((128, 4, 128), np.float32)
    m[:, 0, :] = (k >= q)
    m[:, 1, :] = (k <= q)
    if hf == 1:
        m[:, 2, :] = (k >= q)
    if hf == 0:
        m[:, 3, :] = (k <= q)
    return m


def _qperm():
    idx = np.zeros(1024, np.int64)
    for c in range(8):
        for u in range(2):
            g = 2 * (c // 4) + u
            i = c % 4
            h = g * 4 + i
            idx[c * 128 + u * 64:c * 128 + u * 64 + 64] = h * 64 + np.arange(64)
    return idx


def _fm(v):
    return np.ascontiguousarray(np.asarray(v, np.float32).reshape(-1, 128).T)


_NC = None
_DBG = (DEPTH, 9)
_SUB = 0
_RL = 9


def kernel(x_prompt, x_sample, c, cache_a_k, cache_a_v, cache_b_ckv, cache_b_krope, cache_c_k, cache_c_v,
           c_ctx, w_ada, b_ada, norm_g, w_mlp_in, w_mlp_out, a_w_qkv, a_sink, a_w_o,
           b_w_dq, b_g_q, b_w_uq, b_w_dkv, b_g_kv, b_w_ukv, b_w_o, c_w_qkv, c_g_q, c_g_k, c_w_o, g_final):
    global _NC
    in_maps = _prepare(x_prompt, x_sample, c, cache_a_k, cache_a_v, cache_b_ckv, cache_b_krope, cache_c_k, cache_c_v,
                       c_ctx, w_ada, b_ada, norm_g, w_mlp_in, w_mlp_out, a_w_qkv, a_sink, a_w_o,
                       b_w_dq, b_g_q, b_w_uq, b_w_dkv, b_g_kv, b_w_ukv, b_w_o, c_w_qkv, c_g_q, c_g_k, c_w_o, g_final)
    if _NC is None:
        _NC = build_program(*_DBG)
    res = run_bass_kernel_spmd(_NC, in_maps, core_ids=list(range(8))).results
    return _assemble(res)


def _prepare(x_prompt, x_sample, c, cache_a_k, cache_a_v, cache_b_ckv, cache_b_krope, cache_c_k, cache_c_v,
             c_ctx, w_ada, b_ada, norm_g, w_mlp_in, w_mlp_out, a_w_qkv, a_sink, a_w_o,
             b_w_dq, b_g_q, b_w_uq, b_w_dkv, b_g_kv, b_w_ukv, b_w_o, c_w_qkv, c_g_q, c_g_k, c_w_o, g_final):
    f = lambda a: np.ascontiguousarray(np.asarray(a, np.float32))
    x_prompt, x_sample, c = f(x_prompt), f(x_sample), f(c)
    qp = _qperm()
    qkvp = np.concatenate([qp, 1024 + np.arange(512)])
    a_qkv_p = f(np.asarray(a_w_qkv)[:, :, qkvp])
    a_o_p = f(np.asarray(a_w_o)[:, qp, :])
    c_qkv_p = f(np.asarray(c_w_qkv)[0][:, qkvp])
    c_o_p = f(np.asarray(c_w_o)[0][qp, :])
    shared = {
        "consts": _consts(),
        "w_ada": f(np.asarray(w_ada)[:_DBG[0]]),
        "b_adaT": f(np.asarray(b_ada).reshape(DEPTH, 48, 128).transpose(2, 0, 1)),
        "ngT": f(np.concatenate([np.asarray(norm_g).reshape(DEPTH * 2, 8, 128), np.asarray(g_final).reshape(1, 8, 128)], 0).transpose(2, 0, 1)),
        "w_mlp_in": f(np.asarray(w_mlp_in)[:_DBG[0]]), "w_mlp_out": f(np.asarray(w_mlp_out)[:_DBG[0]]),
        "a_w_qkv": a_qkv_p, "a_w_o": a_o_p,
        "a_sinkb": f(np.broadcast_to(np.asarray(a_sink)[None], (128, 2, 16))),
        "b_w_dq": f(np.asarray(b_w_dq)[0]), "b_w_uq": f(np.asarray(b_w_uq)[0]), "b_w_dkv": f(np.asarray(b_w_dkv)[0]),
        "b_w_ukv": f(np.asarray(b_w_ukv)[0]), "b_w_o": f(np.asarray(b_w_o)[0]),
        "b_gT": f(np.concatenate([_fm(np.asarray(b_g_q)[0]), _fm(np.asarray(b_g_kv)[0])], 1)),
        "c_w_qkv": c_qkv_p, "c_w_o": c_o_p,
        "c_gT": f(np.stack([np.tile(np.asarray(c_g_q)[0], 2), np.tile(np.asarray(c_g_k)[0], 2)], 1)),
    }
    in_maps = []
    for core in range(8):
        p, hf = core // 2, core % 2
        xs = x_sample[p, hf * TS:(hf + 1) * TS]
        xt = np.concatenate([xs, x_prompt[2 * core], x_prompt[2 * core + 1]], 0).T
        m = dict(shared)
        m["xT"] = f(xt)
        m["cvec"] = f(np.stack([_fm(c_ctx), _fm(c[p])], 2))
        m["rope"] = _rope_tables(hf)
        m["masks"] = _masks(hf)
        m["ctx_a_k"] = f(np.asarray(cache_a_k)[p].reshape(2, 256, 256).transpose(0, 2, 1))
        m["ctx_a_v"] = f(np.asarray(cache_a_v)[p].reshape(2, 256, 256))
        m["ctx_b_ckv"] = f(np.asarray(cache_b_ckv)[p, 0].T)
        m["ctx_b_kr"] = f(np.asarray(cache_b_krope)[p, 0].T)
        m["ctx_c_k"] = f(np.asarray(cache_c_k)[p, 0].reshape(256, 256).T)
        m["ctx_c_v"] = f(np.asarray(cache_c_v)[p, 0].reshape(256, 256))
        in_maps.append(m)
    return in_maps


def _assemble(res):
    y_prompt = np.zeros((16, 256, D), np.float32)
    y_sample = np.zeros((4, 4096, D), np.float32)
    s_a_k = np.zeros((16, 2, 256, 4, 64), np.float32)
    s_a_v = np.zeros((16, 2, 256, 4, 64), np.float32)
    s_b_c = np.zeros((16, 1, 256, 256), np.float32)
    s_b_r = np.zeros((16, 1, 256, 32), np.float32)
    s_c_k = np.zeros((16, 1, 256, 4, 64), np.float32)
    s_c_v = np.zeros((16, 1, 256, 4, 64), np.float32)
    for core in range(8):
        r = res[core]
        p, hf = core // 2, core % 2
        y = r["yT"].T
        y_sample[p, hf * TS:(hf + 1) * TS] = y[0:TS]
        for s in range(2):
            b = 2 * core + s
            y_prompt[b] = y[TS + s * 256:TS + (s + 1) * 256]
            sl = slice(s * 256, (s + 1) * 256)
            for jj in range(2):
                s_a_k[b, jj] = r["st_a_k"][jj][:, sl].T.reshape(256, 4, 64)
                s_a_v[b, jj] = r["st_a_v"][jj][sl].reshape(256, 4, 64)
            s_b_c[b, 0] = r["st_b_ckv"][:, sl].T
            s_b_r[b, 0] = r["st_b_kr"][:, sl].T
            s_c_k[b, 0] = r["st_c_k"][:, sl].T.reshape(256, 4, 64)
            s_c_v[b, 0] = r["st_c_v"][sl].reshape(256, 4, 64)
    return (y_prompt, y_sample, s_a_k, s_a_v, s_b_c, s_b_r, s_c_k, s_c_v)
```

```python
import numpy as np
from contextlib import ExitStack
import concourse.bass as bass
import concourse.mybir as mybir
from concourse.bass_utils import run_bass_kernel_spmd

F32, BF = mybir.dt.float32, mybir.dt.bfloat16
ALU = mybir.AluOpType
AF = mybir.ActivationFunctionType

D = 1024
NT = 5
T = 2560
TS = 2048
EPS = 1e-6
THETA = 10000.0
DEPTH = 4
SAME_ENGINE_SYNC = True
CH = 4000


class Op:
    pass


class Prog:
    ENGS = ('pe', 'act', 'dve', 'pool', 'sp')

    def __init__(self):
        self.ops = []

    def add(self, eng, fn, reads=(), writes=(), kind='c', stream=None):
        o = Op()
        reads, writes = tuple(reads), tuple(writes)
        writes = writes + tuple(r for r in reads if isinstance(r, tuple) and r[0] == 'ps' and r not in writes)
        o.eng, o.fn, o.reads, o.writes, o.kind, o.stream = eng, fn, reads, writes, kind, stream
        o.sig = False
        self.ops.append(o)
        return o

    def barrier(self):
        for e in self.ENGS:
            self.add(e, None, kind='bar')

    def analyze(self):
        last_w, readers = {}, {}
        eng_ops = {e: [] for e in self.ENGS}
        stream_cnt, stream_last = {}, {}
        last_c = {}
        for o in self.ops:
            deps = {}

            def add(d, ty):
                if d is not o:
                    deps.setdefault(d, set()).add(ty)
            if o.kind == 'bar':
                for e in self.ENGS:
                    if e != o.eng and e in last_c:
                        add(last_c[e], 'RAW')
                for s, d in stream_last.items():
                    add(d, 'RAW')
            else:
                for r in o.reads:
                    if r in last_w:
                        add(last_w[r], 'RAW')
                for w in o.writes:
                    if w in last_w:
                        add(last_w[w], 'WAW')
                    for rd in readers.get(w, ()):
                        add(rd, 'WAR')
            if o.kind in ('dma', 'cc'):
                prev = stream_last.get(o.stream)
                if prev is not None:
                    add(prev, 'RAW')
                stream_cnt[o.stream] = stream_cnt.get(o.stream, 0) + 1
                o.dman = stream_cnt[o.stream]
                stream_last[o.stream] = o
            o.deps = deps
            for r in o.reads:
                readers.setdefault(r, []).append(o)
            for w in o.writes:
                last_w[w] = o
                readers[w] = []
            if o.kind == 'c':
                last_c[o.eng] = o
            o.eidx = len(eng_ops[o.eng])
            eng_ops[o.eng].append(o)
        self.eng_ops = eng_ops
        self.stream_cnt = stream_cnt
        waited = {e: {d: -1 for d in self.ENGS} for e in self.ENGS}
        waited_s = {e: {} for e in self.ENGS}
        for o in self.ops:
            E = o.eng
            cw = {}
            sw = {}
            for d, tys in o.deps.items():
                if d.kind in ('dma', 'cc'):
                    if d.dman > sw.get(d.stream, (0, None))[0]:
                        sw[d.stream] = (d.dman, d)
                elif d.kind == 'c':
                    if d.eng == E and o.kind == 'c':
                        if E == 'pe' or not SAME_ENGINE_SYNC or not (tys & {'RAW', 'WAW'}):
                            continue
                    if d.eng not in cw or d.eidx > cw[d.eng].eidx:
                        cw[d.eng] = d
            o.cwaits, o.swaits = [], []
            for De, d in cw.items():
                if d.eidx <= waited[E][De]:
                    continue
                if De == E and o.kind == 'c':
                    if any(x.eng != E and x.kind == 'c' and d in x.deps for x in cw.values()):
                        continue
                waited[E][De] = d.eidx
                d.sig = True
                o.cwaits.append(d)
            for s, (n, d) in sw.items():
                if n <= waited_s[E].get(s, 0):
                    continue
                waited_s[E][s] = n
                o.swaits.append(d)
        self.nsig = {}
        for e in self.ENGS:
            k = 0
            for o in eng_ops[e]:
                if o.sig:
                    k += 1
                    o.sigk = k
            self.nsig[e] = k


def build_program(nlayers=DEPTH, stop=9):
    nc = bass.Bass("TRN2", target_bir_lowering=False)
    P = Prog()
    es = ExitStack()

    def din(name, shape, dt=F32):
        return nc.dram_tensor(name, list(shape), dt, kind="ExternalInput").ap()

    def dout(name, shape, dt=F32):
        return nc.dram_tensor(name, list(shape), dt, kind="ExternalOutput").ap()

    xT = din("xT", [D, T])
    cvec = din("cvec", [128, 8, 2])
    rope = din("rope", [4, 128, TS])
    masks = din("masks", [128, 4, 128])
    consts = din("consts", [128, 6, 128])
    w_ada = din("w_ada", [nlayers, D, 6 * D])
    b_adaT = din("b_adaT", [128, DEPTH, 48])
    ngT = din("ngT", [128, DEPTH * 2 + 1, 8])
    w_mlp_in = din("w_mlp_in", [nlayers, D, 4 * D])
    w_mlp_out = din("w_mlp_out", [nlayers, 4 * D, D])
    a_w_qkv = din("a_w_qkv", [2, D, 1536])
    a_w_o = din("a_w_o", [2, D, D])
    a_sinkb = din("a_sinkb", [128, 2, 16])
    b_w_dq = din("b_w_dq", [D, 384])
    b_w_uq = din("b_w_uq", [384, 1536])
    b_w_dkv = din("b_w_dkv", [D, 288])
    b_w_ukv = din("b_w_ukv", [256, 2048])
    b_w_o = din("b_w_o", [D, D])
    b_gT = din("b_gT", [128, 5])
    c_w_qkv = din("c_w_qkv", [D, 1536])
    c_w_o = din("c_w_o", [D, D])
    c_gT = din("c_gT", [128, 2])
    ctx_a_k = din("ctx_a_k", [2, 256, 256])
    ctx_a_v = din("ctx_a_v", [2, 256, 256])
    ctx_b_ckv = din("ctx_b_ckv", [256, 256])
    ctx_b_kr = din("ctx_b_kr", [32, 256])
    ctx_c_k = din("ctx_c_k", [256, 256])
    ctx_c_v = din("ctx_c_v", [256, 256])

    yT = dout("yT", [D, T])
    st_a_k = dout("st_a_k", [2, 256, 512])
    st_a_v = dout("st_a_v", [2, 512, 256])
    st_b_ckv = dout("st_b_ckv", [256, 512])
    st_b_kr = dout("st_b_kr", [32, 512])
    st_c_k = dout("st_c_k", [256, 512])
    st_c_v = dout("st_c_v", [512, 256])

    def dint(name, shape):
        return nc.dram_tensor(name, list(shape), BF)
    exAkI = [dint(f"exAki{j}", [256, 256]) for j in range(2)]
    exAkO = [dint(f"exAko{j}", [512, 256]) for j in range(2)]
    exAvI = [dint(f"exAvi{j}", [256, 256]) for j in range(2)]
    exAvO = [dint(f"exAvo{j}", [512, 256]) for j in range(2)]
    exK_in = {l: dint(f"exKi{l}", [256, TS]) for l in ('B', 'C')}
    exK_out = {l: dint(f"exKo{l}", [512, TS]) for l in ('B', 'C')}
    exV_in = dint("exVi", [TS, 256])
    exV_out = dint("exVo", [2 * TS, 256])
    exR_in = dint("exRi", [32, TS])
    exR_out = dint("exRo", [64, TS])

    def sb(name, shape, dt):
        return es.enter_context(nc.sbuf_tensor(name, list(shape), dt))
    X = sb("X", [128, 8, T], F32)
    H = sb("H", [128, 8, T], BF)
    SCR = sb("SCR", [128, 4, 512], F32)
    SQB = sb("SQB", [128, 2, 512], BF)
    MODSL = [sb("MODS0", [128, 2, 64], F32), sb("MODS1", [128, 2, 64], F32)]
    CB = sb("CB", [128, 6, 128], BF)
    ONESB = sb("ONESB", [128, 128], BF)
    ONESF = sb("ONESF", [128, 128], F32)
    MASK = sb("MASK", [128, 4, 128], BF)
    NG = sb("NG", [128, DEPTH * 2 + 1, 8], F32)
    BADA = sb("BADA", [128, DEPTH, 48], F32)
    BG = sb("BG", [128, 5], F32)
    CG = sb("CG", [128, 2], F32)
    ESINK = sb("ESINK", [128, 2, 16], F32)
    SILU = sb("SILU", [128, 8, 2], BF)
    CV = sb("CV", [128, 8, 2], F32)
    EPSC = sb("EPSC", [128, 1], F32)
    ZEROC = sb("ZEROC", [128, 1], F32)
    ROPE = sb("ROPE", [128, 2, 512], F32)
    ARENA_N = 34304
    ARENA = sb("ARENA", [128, ARENA_N], BF)
    ps = [es.enter_context(nc.psum_tensor(f"ps{i}", [128, 512], F32)) for i in range(8)]

    IDENT = CB[:, 0, :]
    PERM64 = CB[:, 1, :]
    PERMM = CB[:, 2, :]
    BDIAG = CB[:, 3, :]

    def arena(off, n):
        assert off + n <= ARENA_N, (off, n)
        return ARENA[:, off:off + n]

    wq_ctr = [0]
    NW = 3

    def dma(queue, out, in_, reads, writes, stream):
        P.add(queue, lambda e, o=out, i=in_: e.dma_start(out=o, in_=i), reads, writes, kind='dma', stream=queue + '_' + stream)

    io_ctr = [0]

    def io_stream():
        io_ctr[0] += 1
        return f"io{io_ctr[0] % 6}"

    def mm(out, lhsT, rhs, start, stop, reads, writes, skip=False):
        if skip:
            P.add('pe', lambda e, o=out, l=lhsT, r=rhs, s=start, t=stop: e.matmul(o, l, r, start=s, stop=t, skip_group_check=True), reads, writes)
        else:
            P.add('pe', lambda e, o=out, l=lhsT, r=rhs, s=start, t=stop: e.matmul(o, l, r, start=s, stop=t), reads, writes)

    def act(out, in_, func, reads, writes, bias=None, scale=None):
        kw = {}
        if bias is None:
            p0 = out.base_partition()
            bias = ZEROC[p0:p0 + out.shape[0], 0:1]
        kw['bias'] = bias
        if scale is not None:
            kw['scale'] = scale
        P.add('act', lambda e, o=out, i=in_, f=func, k=kw: e.activation(out=o, in_=i, func=f, **k), reads, writes)

    def tt(eng, out, in0, in1, op, reads, writes):
        P.add(eng, lambda e, o=out, a=in0, b=in1, p=op: e.tensor_tensor(out=o, in0=a, in1=b, op=p), reads, writes)

    def stt(eng, out, in0, scalar, in1, op0, op1, reads, writes):
        P.add(eng, lambda e, o=out, a=in0, s=scalar, b=in1, p=op0, q=op1:
              e.scalar_tensor_tensor(out=o, in0=a, scalar=s, in1=b, op0=p, op1=q), reads, writes)

    def ts(eng, out, in0, s1, s2, op0, op1, reads, writes):
        if s2 is None:
            P.add(eng, lambda e, o=out, a=in0, s=s1, p=op0: e.tensor_scalar(out=o, in0=a, scalar1=s, scalar2=None, op0=p), reads, writes)
        else:
            P.add(eng, lambda e, o=out, a=in0, s=s1, u=s2, p=op0, q=op1:
                  e.tensor_scalar(out=o, in0=a, scalar1=s, scalar2=u, op0=p, op1=q), reads, writes)

    def recip(out, in_, reads, writes):
        P.add('dve', lambda e, o=out, i=in_: e.reciprocal(out=o, in_=i), reads, writes)

    def cp(eng, out, in_, reads, writes):
        if eng == 'act':
            act(out, in_, AF.Identity, reads, writes)
        else:
            P.add(eng, lambda e, o=out, i=in_: e.tensor_copy(out=o, in_=i), reads, writes)

    def memset(eng, ap, val, writes):
        P.add(eng, lambda e, a=ap, v=val: e.memset(a, v), (), writes)

    def tsl(t):
        return slice(t * 512, (t + 1) * 512)

    rot = {}

    def rotate(name, banks):
        i = rot.get(name, 0)
        rot[name] = i + 1
        return banks[i % len(banks)]

    def scr(i):
        return SCR[:, i, :]

    def wload(view, src, tag):
        wq_ctr[0] += 1
        dma('pool', view, src.rearrange("(k p) n -> p k n", p=128), (), (tag,), f"w{wq_ctr[0] % 6}")

    memset('dve', ONESB[:], 1.0, ('ONESB',))
    memset('dve', ONESF[:], 1.0, ('ONESF',))
    memset('dve', EPSC[:], EPS, ('EPSC',))
    memset('dve', ZEROC[:], 0.0, ('ZEROC',))
    for k in range(8):
        dma('sp', X[:, k, :], xT[k * 128:(k + 1) * 128, :], (), ('X',), f'io{k % 3}')
    dma('pool', CB[:], consts, (), ('CB',), 'io1')
    dma('pool', MASK[:], masks, (), ('MASK',), 'io2')
    dma('sp', NG[:], ngT, (), ('NG',), 'io3')
    dma('sp', BADA[:], b_adaT, (), ('BADA',), 'io4')
    dma('sp', BG[:], b_gT, (), ('BG',), 'io5')
    dma('sp', CG[:], c_gT, (), ('CG',), 'io3')
    dma('sp', ESINK[:], a_sinkb, (), ('ESINK',), 'io4')
    dma('sp', CV[:], cvec, (), ('CV',), 'io5')
    act(ESINK[:], ESINK[:], AF.Exp, ('ESINK',), ('ESINK',))
    act(SILU[:], CV[:], AF.Silu, ('CV',), ('SILU',))
    P.barrier()

    AW_OFF = 28672

    def ada_dma(l, n):
        W = arena(AW_OFF + (n % 2) * 2048, 2048).rearrange("p (k n) -> p k n", k=8)
        wload(W, w_ada[l][:, n * 256:(n + 1) * 256], ('AW', n % 2))

    def ada_mm(l, n):
        W = arena(AW_OFF + (n % 2) * 2048, 2048).rearrange("p (k n) -> p k n", k=8)
        pm = ps[7]
        for mm_ in range(2):
            m = n * 2 + mm_
            for k in range(8):
                mm(pm[:, 2 * m:2 * m + 2], W[:, k, mm_ * 128:(mm_ + 1) * 128], SILU[:, k, :], k == 0, k == 7,
                   (('AW', n % 2), 'SILU'), (('ps', id(pm)),))

    def ada_finish(l):
        MODS = MODSL[l % 2]
        mt = ('MODS', l % 2)
        pm = ps[7]
        tt('dve', MODS[:, :, 0:48].rearrange("p j m -> p m j"),
           pm[:, 0:96].rearrange("p (m j) -> p m j", j=2),
           BADA[:, l, :].unsqueeze(2).to_broadcast([128, 48, 2]), ALU.add, (('ps', id(pm)), 'BADA'), (mt,))
        for which in range(2):
            stt('dve', MODS[:, :, 48 + 8 * which:56 + 8 * which], MODS[:, :, 8 + 24 * which:16 + 24 * which], 1.0,
                NG[:, 2 * l + which, :].unsqueeze(1).to_broadcast([128, 2, 8]), ALU.add, ALU.mult,
                (mt, 'NG'), (mt,))

    def ada_mods(l):
        for n in range(24):
            ada_dma(l, n)
            ada_mm(l, n)
        ada_finish(l)

    def rms_stats(src_fn, nchunks, ncols, inv_n, lhsT, tagsrc, bank):
        pss = bank
        for k in range(nchunks):
            sq = SQB[:, k % 2, 0:ncols]
            act(sq, src_fn(k), AF.Square, tagsrc(k), (('SQB', k % 2),))
            mm(pss[:, 0:ncols], lhsT, sq, k == 0, k == nchunks - 1, (('SQB', k % 2), 'CB', 'ONESB'), (('ps', id(bank)),))
        r = SCR[:, 0, 0:ncols]
        act(r, pss[:, 0:ncols], AF.Sqrt, (('ps', id(bank)), 'EPSC'), (('SCR', 0),), bias=EPSC[:, 0:1], scale=inv_n)
        recip(r, r, (('SCR', 0),), (('SCR', 0),))
        return r

    def norm_tile(t, which, l, out_final=None):
        MODS = MODSL[l % 2]
        j = 0 if t == 4 else 1
        cols = tsl(t)
        r = rms_stats(lambda k: X[:, k, cols], 8, 512, 1.0 / D, ONESB[:], lambda k: (('X', t),), ps[6])
        for k in range(8):
            tmp = scr(1 + k % 2)
            tt('dve', tmp, X[:, k, cols], r, ALU.mult, (('X', t), ('SCR', 0)), (('SCR', 1 + k % 2),))
            if out_final is None:
                act(H[:, k, cols], tmp, AF.Identity, (('SCR', 1 + k % 2), ('MODS', l % 2)), (('H', t),),
                    bias=MODS[:, j, 24 * which + k:24 * which + k + 1], scale=MODS[:, j, 48 + 8 * which + k:49 + 8 * which + k])
            else:
                o = scr(3)
                act(o, tmp, AF.Identity, (('SCR', 1 + k % 2), 'NG'), (('SCR', 3),), scale=NG[:, 2 * DEPTH, k:k + 1])
                dma('sp', yT[k * 128:(k + 1) * 128, cols], o, (('SCR', 3),), (), io_stream())

    def norm_phase(which, l):
        MODS = MODSL[l % 2]
        mtag = ('MODS', l % 2)

        def bank_of(t):
            return ps[6] if t % 2 == 0 else ps[7]

        def rslot(t):
            return 0 if t % 2 == 0 else 3

        def stats_step(t, k):
            cols = tsl(t)
            b = bank_of(t)
            sq = SQB[:, k % 2, :]
            act(sq, X[:, k, cols], AF.Square, (('X', t),), (('SQB', k % 2),))
            mm(b[:], ONESB[:], sq, k == 0, k == 7, (('SQB', k % 2), 'ONESB'), (('ps', id(b)),))

        def stats_fin(t):
            b = bank_of(t)
            r = scr(rslot(t))
            act(r, b[:], AF.Sqrt, (('ps', id(b)), 'EPSC'), (('SCR', rslot(t)),), bias=EPSC[:, 0:1], scale=1.0 / D)
            recip(r, r, (('SCR', rslot(t)),), (('SCR', rslot(t)),))

        def mod_step(t, k):
            j = 0 if t == 4 else 1
            cols = tsl(t)
            tmp = scr(1 + k % 2)
            tt('dve', tmp, X[:, k, cols], scr(rslot(t)), ALU.mult, (('X', t), ('SCR', rslot(t))), (('SCR', 1 + k % 2),))
            act(H[:, k, cols], tmp, AF.Identity, (('SCR', 1 + k % 2), mtag), (('H', t),),
                bias=MODS[:, j, 24 * which + k:24 * which + k + 1], scale=MODS[:, j, 48 + 8 * which + k:49 + 8 * which + k])

        for k in range(8):
            stats_step(0, k)
        stats_fin(0)
        for t in range(NT):
            for k in range(8):
                if t + 1 < NT:
                    stats_step(t + 1, k)
                mod_step(t, k)
            if t + 1 < NT:
                stats_fin(t + 1)

    def out_proj(l, w_dram, gate_off):
        MODS = MODSL[l % 2]
        WT = [arena(i * 4096, 4096).rearrange("p (k n) -> p k n", k=8) for i in range(NW)]
        for wt in range(2):
            W = WT[wt % NW]
            tag = ('WT', wt % NW)
            wload(W, w_dram[:, wt * 512:(wt + 1) * 512], tag)
            for t in range(NT):
                j = 0 if t == 4 else 1
                for mm_ in range(4):
                    m = wt * 4 + mm_
                    pb = rotate('proj', [ps[4], ps[5]])
                    for k in range(8):
                        mm(pb[:], W[:, k, mm_ * 128:(mm_ + 1) * 128], H[:, k, tsl(t)], k == 0, k == 7,
                           (tag, ('H', t)), (('ps', id(pb)),))
                    stt('dve', X[:, m, tsl(t)], pb[:], MODS[:, j, gate_off + m:gate_off + m + 1], X[:, m, tsl(t)],
                        ALU.mult, ALU.add, (('ps', id(pb)), ('MODS', l % 2), ('X', t)), (('X', t),))

    def mlp(l):
        MODS = MODSL[l % 2]
        NWP = 3
        WI = [arena(i * 8192, 4096).rearrange("p (k n) -> p k n", k=8) for i in range(NWP)]
        WO = [arena(i * 8192 + 4096, 4096).rearrange("p (k n) -> p k n", k=4) for i in range(NWP)]
        U = [arena(NWP * 8192 + i * 2048, 2048).rearrange("p (k n) -> p k n", k=4) for i in range(2)]

        def load(pc):
            wload(WI[pc % NWP], w_mlp_in[l][:, pc * 512:(pc + 1) * 512], ('WI', pc % NWP))
            wload(WO[pc % NWP], w_mlp_out[l][pc * 512:(pc + 1) * 512, :], ('WO', pc % NWP))

        def p1(pc, t, ui):
            wi, ti, u = WI[pc % NWP], ('WI', pc % NWP), U[ui]
            for jj in range(4):
                pb = rotate('mlp1', [ps[0], ps[1]])
                for k in range(8):
                    mm(pb[:], wi[:, k, jj * 128:(jj + 1) * 128], H[:, k, tsl(t)], k == 0, k == 7,
                       (ti, ('H', t)), (('ps', id(pb)),))
                si = rotate('mlpscr', [1, 2])
                act(scr(si), pb[:], AF.Relu, (('ps', id(pb)),), (('SCR', si),))
                tt('dve', u[:, jj, :], scr(si), scr(si), ALU.mult, (('SCR', si),), (('U', ui, jj),))

        def p2(pc, t, ui):
            wo, to, u = WO[pc % NWP], ('WO', pc % NWP), U[ui]
            j = 0 if t == 4 else 1
            for m in range(8):
                pb = rotate('mlp2', [ps[2], ps[3]])
                for jj in range(4):
                    mm(pb[:], wo[:, jj, m * 128:(m + 1) * 128], u[:, jj, :], jj == 0, jj == 3,
                       (to, ('U', ui, jj)), (('ps', id(pb)),))
                stt('dve', X[:, m, tsl(t)], pb[:], MODS[:, j, 40 + m:41 + m], X[:, m, tsl(t)],
                    ALU.mult, ALU.add, (('ps', id(pb)), ('MODS', l % 2), ('X', t)), (('X', t),))

        steps = [(pc, t) for pc in range(8) for t in range(NT)]
        load(0)
        load(1)
        for i, (pc, t) in enumerate(steps):
            if t == 0 and 1 <= pc < 7:
                load(pc + 1)
            p1(pc, t, i % 2)
            if i > 0:
                p2(steps[i - 1][0], steps[i - 1][1], (i - 1) % 2)
            if l + 1 < nlayers:
                if 1 <= i <= 24:
                    ada_mm(l + 1, i - 1)
                if i < 24:
                    ada_dma(l + 1, i)
        p2(steps[-1][0], steps[-1][1], (len(steps) - 1) % 2)
        if l + 1 < nlayers:
            ada_finish(l + 1)

    PT = [None] * 3
    ONB = [None]

    def attn_job(kts, rhs, N, nsub, scale, sink, dest, u, qtags, PTb, ONBb, htag):
        pO = rotate('pO', [ps[3], ps[4]])
        otag = ('ps', id(pO))
        sub = N // nsub
        nkt = len(kts)
        LOOK = 2
        banks = {}

        def qk(i):
            lhsT, vaug, mask, ktags = kts[i]
            pS = rotate('pS', [ps[0], ps[1], ps[2]])
            banks[i] = pS
            so = pS[:, 0:N] if len(rhs.shape) == 2 else pS[:, 0:N].rearrange("p (h q) -> p h q", q=128)
            mm(so, lhsT, rhs, True, True, tuple(ktags) + tuple(qtags), (('ps', id(pS)),))
        for i in range(min(LOOK, nkt)):
            qk(i)
        for i, (lhsT, vaug, mask, ktags) in enumerate(kts):
            if i == min(2, nkt - 1):
                attn_flush()
            if i + LOOK < nkt:
                qk(i + LOOK)
            pS = banks.pop(i)
            stag = ('ps', id(pS))
            pi = rot.get('PT', 0) % 3
            rot['PT'] = rot.get('PT', 0) + 1
            pt = PTb[pi]
            ptag = ('PT', pi)
            act(pt[:, 0:N], pS[:, 0:N], AF.Exp, (stag,), (ptag,), scale=scale)
            if mask is not None:
                tt('dve', pt[:, 0:N].rearrange("p (h q) -> p h q", q=128), pt[:, 0:N].rearrange("p (h q) -> p h q", q=128),
                   mask.unsqueeze(1).to_broadcast([128, N // 128, 128]), ALU.mult, (ptag, 'MASK'), (ptag,))
            for s in range(nsub):
                if nsub == 1:
                    mm(pO[0:65, 0:N], vaug, pt[:, 0:N], i == 0, i == len(kts) - 1, (ptag,) + tuple(ktags), (otag,))
                else:
                    mm(pO[0:65, s * sub:(s + 1) * sub], vaug, pt[:, s * sub:(s + 1) * sub], i == 0 and s == 0, False,
                       (ptag,) + tuple(ktags), (otag,), skip=True)
        pending.append(lambda: attn_epilogue(pO, otag, N, sink, dest, u, ONBb, htag))

    pending = []

    def attn_flush():
        lists = [p() if callable(p) else p for p in pending]
        del pending[:]
        while any(lists):
            for L in lists:
                if L:
                    L.pop(0)()

    def attn_epilogue(pO, otag, N, sink, dest, u, ONBb, htag, ebanks=None, slots=(0, 1)):
        ebanks = ebanks or [ps[5], ps[6]]
        ds, os_ = slots
        den = SCR[64:65, ds, 0:N]
        dtag_, ostag = ('SCR', ds), ('SCR', os_)
        osb = SCR[0:64, os_, 0:N]
        three = len(dest.shape) != 2
        st = []
        box = {}

        def e0():
            if sink is not None:
                tt('dve', den.rearrange("p (h q) -> p h q", q=128), pO[64:65, 0:N].rearrange("p (h q) -> p h q", q=128),
                   sink.unsqueeze(2).to_broadcast([1, N // 128, 128]), ALU.add, (otag, 'ESINK'), (dtag_,))
                recip(den, den, (dtag_,), (dtag_,))
            else:
                recip(den, pO[64:65, 0:N], (otag,), (dtag_,))
        st.append(e0)

        def e1():
            pB = rotate('pB', ebanks)
            box['pB'] = pB
            mm(pB[0:64, 0:N], ONESF[64:65, 0:64], den, True, True, (dtag_, 'ONESF'), (('ps', id(pB)),))
            cp('act', osb, pO[0:64, 0:N], (otag,), (ostag,))
        st.append(e1)

        def e2():
            pB = box['pB']
            btag = ('ps', id(pB))
            if u == 0:
                tt('dve', dest, osb.rearrange("p (h q) -> p h q", q=128) if three else osb,
                   pB[0:64, 0:N].rearrange("p (h q) -> p h q", q=128) if three else pB[0:64, 0:N],
                   ALU.mult, (ostag, btag), (htag,))
            else:
                tt('dve', ONBb[0:64, 0:N], osb, pB[0:64, 0:N], ALU.mult, (ostag, btag), ('ONB',))
        st.append(e2)
        if u == 1:
            def e3():
                pM = rotate('pB', ebanks)
                box['pM'] = pM
                mm(pM[64:128, 0:N], IDENT[0:64, 0:64], ONBb[0:64, 0:N], True, True, ('ONB', 'CB'), (('ps', id(pM)),))
            st.append(e3)

            def e4():
                pM = box['pM']
                cp('act', dest, pM[64:128, 0:N].rearrange("p (h q) -> p h q", q=128) if three else pM[64:128, 0:N],
                   (('ps', id(pM)),), (htag,))
            st.append(e4)
        return st

    def attn_pair(jobs, N, nsub, scale, qtags, PTb, ONBb, side=()):
        nkt = len(jobs[0]['kts'])
        sub = N // nsub
        PT4 = [PTb[0], PTb[1], PTb[2], SQB[:, 1, :]]
        PT4t = [('PT', 0), ('PT', 1), ('PT', 2), ('SQB', 1)]
        pSb = [[ps[0], ps[1]], [ps[2], ps[3]]]
        pOb = [ps[4], ps[5]]

        def qk(x, i):
            jb = jobs[x]
            lhsT, vaug, mask, ktags = jb['kts'][i]
            pS = pSb[x][i % 2]
            rhs = jb['rhs']
            so = pS[:, 0:N] if len(rhs.shape) == 2 else pS[:, 0:N].rearrange("p (h q) -> p h q", q=128)
            mm(so, lhsT, rhs, True, mask is None, tuple(ktags) + tuple(qtags), (('ps', id(pS)),))

        def qmask(x, i):
            lhsT, vaug, mask, ktags = jobs[x]['kts'][i]
            if mask is not None:
                pS = pSb[x][i % 2]
                mm(pS[:, 0:N], IDENT, mask, False, True, ('MNEG', 'CB'), (('ps', id(pS)),))
        qk(0, 0)
        qk(1, 0)
        qmask(0, 0)
        qmask(1, 0)
        for i in range(nkt):
            if i + 1 < nkt:
                qk(0, i + 1)
                qk(1, i + 1)
                qmask(0, i + 1)
                qmask(1, i + 1)
            for x in range(2):
                lhsT, vaug, mask, ktags = jobs[x]['kts'][i]
                pS = pSb[x][i % 2]
                pt, ptag = PT4[2 * x + i % 2], PT4t[2 * x + i % 2]
                act(pt[:, 0:N], pS[:, 0:N], AF.Exp, (('ps', id(pS)),), (ptag,), scale=scale)
            if i == 0 and nkt > 1:
                continue
            if i == min(1, nkt - 1):
                attn_flush()
            for ii in ([0, 1] if (i == 1 and nkt > 1) else [i]):
                for x in range(2):
                    lhsT, vaug, mask, ktags = jobs[x]['kts'][ii]
                    pt, ptag = PT4[2 * x + ii % 2], PT4t[2 * x + ii % 2]
                    pO = pOb[x]
                    otag = ('ps', id(pO))
                    for s_ in range(nsub):
                        if nsub == 1:
                            mm(pO[0:65, 0:N], vaug, pt[:, 0:N], ii == 0, ii == nkt - 1, (ptag,) + tuple(ktags), (otag,))
                        else:
                            mm(pO[0:65, s_ * sub:(s_ + 1) * sub], vaug, pt[:, s_ * sub:(s_ + 1) * sub], ii == 0 and s_ == 0, False,
                               (ptag,) + tuple(ktags), (otag,), skip=True)
            if side and i >= 1:
                side.pop(0)()
        for x in range(2):
            jb = jobs[x]
            pending.append(lambda pO=pOb[x], jb=jb, x=x: attn_epilogue(pO, ('ps', id(pO)), N, jb['sink'], jb['dest'], jb['u'], ONBb, jb['htag'],
                                                                   ebanks=[ps[6], ps[7]], slots=((0, 1) if x == 0 else (3, 2))))

    def rope_apply(src_ps, rows, ncols, cos, sin, perm, dst, srctag, dsttags, p2bank=None):
        t1 = SCR[rows, 1, 0:ncols]
        t2 = SCR[rows, 2, 0:ncols]
        kb = SQB[rows, 0, 0:ncols]
        if _RL < 1:
            return
        tt('dve', t1, src_ps, cos, ALU.mult, (srctag, 'ROPE'), (('SCR', 1),))
        if _RL < 2:
            return
        cp('act', kb, src_ps, (srctag,), (('SQB', 0),))
        if _RL < 3:
            return
        p2 = p2bank if p2bank is not None else rotate('pB', [ps[5], ps[6]])
        mm(p2[rows, 0:ncols], perm, kb, True, True, (('SQB', 0), 'CB'), (('ps', id(p2)),))
        if _RL < 4:
            return
        tt('dve', t2, p2[rows, 0:ncols], sin, ALU.mult, (('ps', id(p2)), 'ROPE'), (('SCR', 2),))
        tt('dve', dst, t1, t2, ALU.add, (('SCR', 1), ('SCR', 2)), dsttags)

    def gqa_layer(l, kind, j, w_qkv, w_o, ctx_k, ctx_v, st_k, st_v):
        isA = kind == 'A'
        if isA:
            NK, NKT = 3072, 24
            K_HL, K_HR, K_CTX, K_P = 2048, 2176, 2304, 2560
        else:
            NK, NKT = 4864, 38
            K_CTX, K_P = 4096, 4352
        off = 0
        KB = arena(off, 2 * NK).rearrange("p (c n) -> p c n", c=2); off += 2 * NK
        VB = arena(off, NKT * 264).rearrange("p (t n) -> p t n", n=264); off += NKT * 264
        WT = [arena(off, 4096).rearrange("p (k n) -> p k n", k=8)]; off += 4096
        QTB = [arena(off + i * 4096, 4096).rearrange("p (c n) -> p c n", c=8) for i in range(2)]; off += 8192
        PTb = [arena(off + i * 512, 512) for i in range(3)]; off += 1536
        ONBb = arena(off, 512); off += 512
        VB4 = VB.rearrange("p t (g e) -> p t g e", e=66)
        if isA:
            MNEG = arena(off, 2048).rearrange("p (m n) -> p m n", m=4); off += 2048
            for m_ in range(4):
                ts('dve', MNEG[:, m_, :].rearrange("p (h q) -> p h q", q=128), MASK[:, m_, :].unsqueeze(1).to_broadcast([128, 4, 128]),
                   -1.0, 30000.0, ALU.add, ALU.mult, ('MASK',), ('MNEG',))
        memset('dve', VB[:], 1.0, ('VB',))
        W = WT[0]
        wload(W, w_qkv[:, 1024:1536], ('WT', 0))
        dma('pool', KB[:, :, K_CTX:K_CTX + 256], ctx_k.rearrange("(c p) t -> p c t", p=128), (), ('KB',), 'io0')
        for s_ in range(2):
            dma('pool', VB4[:, K_CTX // 128 + s_, :, 0:64],
                ctx_v[s_ * 128:(s_ + 1) * 128, :].rearrange("p (g d) -> p g d", d=64), ('VB',), ('VB',), 'io1')
        if _SUB == 11:
            return
        for t in range(NT):
            if _SUB in (12, 13) and t > 0:
                return
            if _SUB == 14 and t < 4:
                continue
            samp = t < 4
            cols = tsl(t)
            if samp:
                dma('sp', ROPE[:, :, :], rope[0:2, :, cols].rearrange("a p n -> p a n"), (), ('ROPE',), 'io2')
            kcol0 = t * 512 if samp else K_P
            for c in range(2):
                pb = rotate('proj', [ps[6], ps[7]])
                ptag = ('ps', id(pb))
                for k in range(8):
                    mm(pb[:], W[:, k, c * 128:(c + 1) * 128], H[:, k, cols], k == 0, k == 7, (('WT', 0), ('H', t)), (ptag,))
                src = pb[:]
                srctag = ptag
                if not isA:
                    sq = SQB[:, 1, :]
                    act(sq, pb[:], AF.Square, (ptag,), (('SQB', 1),))
                    pn = rotate('pB', [ps[5], ps[6]]) if False else ps[4]
                    mm(pn[:], BDIAG, sq, True, True, (('SQB', 1), 'CB'), (('ps', id(pn)),))
                    r = scr(0)
                    act(r, pn[:], AF.Sqrt, (('ps', id(pn)), 'EPSC'), (('SCR', 0),), bias=EPSC[:, 0:1], scale=1.0 / 64)
                    recip(r, r, (('SCR', 0),), (('SCR', 0),))
                    kn = scr(3)
                    stt('dve', kn, pb[:], CG[:, 1:2], r, ALU.mult, ALU.mult, (ptag, 'CG', ('SCR', 0)), (('SCR', 3),))
                    src = kn
                    srctag = ('SCR', 3)
                dstK = KB[:, c, kcol0:kcol0 + 512]
                if samp:
                    rope_apply(src, slice(0, 128), 512, ROPE[:, 0, :], ROPE[:, 1, :], PERM64, dstK, srctag, ('KB',))
                else:
                    if isA:
                        kf = scr(3)
                        cp('dve', kf, pb[:], (ptag,), (('SCR', 3),))
                    else:
                        kf = src
                    cp('act', dstK, kf, (('SCR', 3),), ('KB',))
                    dma('sp', st_k[c * 128:(c + 1) * 128, :], kf, (('SCR', 3),), (), io_stream())
            for s in range(4):
                if _SUB == 12:
                    break
                tok = slice(t * 512 + s * 128, t * 512 + (s + 1) * 128)
                pb = rotate('proj', [ps[6], ps[7]])
                ptag = ('ps', id(pb))
                for k in range(8):
                    mm(pb[:, 0:256], H[:, k, tok], W[:, k, 256:512], k == 0, k == 7, (('WT', 0), ('H', t)), (ptag,))
                kt = (kcol0 // 128) + s
                cp('act', VB4[:, kt, :, 0:64], pb[:, 0:256].rearrange("p (g d) -> p g d", d=64), (ptag,), ('VB',))
                if not samp:
                    vf = scr(2)
                    cp('dve', vf[:, 0:256], pb[:, 0:256], (ptag,), (('SCR', 2),))
                    dma('sp', st_v[s * 128:(s + 1) * 128, :], vf[:, 0:256], (('SCR', 2),), (), io_stream())
        if _SUB == 1:
            return
        if isA:
            ki, ko, vi, vo = exAkI[j], exAkO[j], exAvI[j], exAvO[j]
            for c in range(2):
                dma('sp', ki[c * 128:(c + 1) * 128, 0:128], KB[:, c, 0:128], ('KB',), ('exki',), 'ex0')
                dma('sp', ki[c * 128:(c + 1) * 128, 128:256], KB[:, c, 1920:2048], ('KB',), ('exki',), 'ex1')
            dma('sp', vi[0:128, :].rearrange("p (g d) -> p g d", d=64), VB4[:, 0, :, 0:64], ('VB',), ('exvi',), 'ex2')
            dma('sp', vi[128:256, :].rearrange("p (g d) -> p g d", d=64), VB4[:, 15, :, 0:64], ('VB',), ('exvi',), 'ex3')
        else:
            ki, ko, vi, vo = exK_in['C'], exK_out['C'], exV_in, exV_out
            for c in range(2):
                dma('sp', ki[c * 128:(c + 1) * 128, :], KB[:, c, 0:TS], ('KB',), ('exki',), f'ex{c}')
            for s_ in range(16):
                dma('sp', vi[s_ * 128:(s_ + 1) * 128, :].rearrange("p (g d) -> p g d", d=64), VB4[:, s_, :, 0:64], ('VB',), ('exvi',), f'ex{2 + s_ % 4}')
        P.add('pool', lambda e, a=ki, b=ko: e.collective_compute("AllGather", ALU.bypass, replica_groups=[[0, 1], [2, 3], [4, 5], [6, 7]],
                                                                 ins=[a.ap().opt()], outs=[b.ap().opt()]),
              ('exki',), ('exko',), kind='cc', stream=f'cc{l}k')
        P.add('pool', lambda e, a=vi, b=vo: e.collective_compute("AllGather", ALU.bypass, replica_groups=[[0, 1], [2, 3], [4, 5], [6, 7]],
                                                                 ins=[a.ap().opt()], outs=[b.ap().opt()]),
              ('exvi',), ('exvo',), kind='cc', stream=f'cc{l}v')
        if isA:
            for c in range(2):
                dma('sp', KB[:, c, K_HL:K_HL + 128], ko[c * 128:(c + 1) * 128, 128:256], ('exko',), ('KB',), 'ex0')
                dma('sp', KB[:, c, K_HR:K_HR + 128], ko[256 + c * 128:256 + (c + 1) * 128, 0:128], ('exko',), ('KB',), 'ex1')
            dma('sp', VB4[:, 16, :, 0:64], vo[128:256, :].rearrange("p (g d) -> p g d", d=64), ('exvo',), ('VB',), 'ex2')
            dma('sp', VB4[:, 17, :, 0:64], vo[256:384, :].rearrange("p (g d) -> p g d", d=64), ('exvo',), ('VB',), 'ex3')
        else:
            for r in range(2):
                for c in range(2):
                    dma('sp', KB[:, c, r * TS:(r + 1) * TS], ko[r * 256 + c * 128:r * 256 + (c + 1) * 128, :], ('exko',), ('KB',), f'ex{c}')
                for s_ in range(16):
                    dma('sp', VB4[:, r * 16 + s_, :, 0:64],
                        vo[r * TS + s_ * 128:r * TS + (s_ + 1) * 128, :].rearrange("p (g d) -> p g d", d=64), ('exvo',), ('VB',), f'ex{2 + s_ % 4}')
        if _SUB == 2:
            return
        scale = 0.125

        def qproj_stages(t, c):
            samp = t < 4
            cols = tsl(t)
            QTn = QTB[t % 2]
            Wq, wtag = WT[0], ('WT', 0)
            mm_ = c % 4
            pb = ps[6]
            ptag = ('ps', id(pb))
            dtag = (('QT', t % 2, c),)
            st = []

            def s_proj():
                if c == 0 and samp:
                    dma('sp', ROPE[:, :, :], rope[0:2, :, cols].rearrange("a p n -> p a n"), (), ('ROPE',), 'io2')
                for k in range(8):
                    mm(pb[:], Wq[:, k, mm_ * 128:(mm_ + 1) * 128], H[:, k, cols], k == 0, k == 7, (wtag, ('H', t)), (ptag,))
            st.append(s_proj)
            src, srctag = pb[:], ptag
            if not isA:
                sq = SQB[:, 0, :]
                pn = ps[7]
                r = scr(0)
                qn = scr(3)
                st.append(lambda: act(sq, pb[:], AF.Square, (ptag,), (('SQB', 0),)))
                st.append(lambda: mm(pn[:], BDIAG, sq, True, True, (('SQB', 0), 'CB'), (('ps', id(pn)),)))
                st.append(lambda: act(r, pn[:], AF.Sqrt, (('ps', id(pn)), 'EPSC'), (('SCR', 0),), bias=EPSC[:, 0:1], scale=1.0 / 64))

                def s_qn():
                    recip(r, r, (('SCR', 0),), (('SCR', 0),))
                    stt('dve', qn, pb[:], CG[:, 0:1], r, ALU.mult, ALU.mult, (ptag, 'CG', ('SCR', 0)), (('SCR', 3),))
                st.append(s_qn)
                src, srctag = qn, ('SCR', 3)
            if samp:
                t1, t2, kb, p2 = scr(1), scr(2), SQB[:, 0, :], ps[7]

                def s_r1():
                    tt('dve', t1, src, ROPE[:, 0, :], ALU.mult, (srctag, 'ROPE'), (('SCR', 1),))
                    cp('act', kb, src, (srctag,), (('SQB', 0),))
                st.append(s_r1)
                st.append(lambda: mm(p2[:], PERM64, kb, True, True, (('SQB', 0), 'CB'), (('ps', id(p2)),)))

                def s_r3():
                    tt('dve', t2, p2[:], ROPE[:, 1, :], ALU.mult, (('ps', id(p2)), 'ROPE'), (('SCR', 2),))
                    tt('dve', QTn[:, c, :], t1, t2, ALU.add, (('SCR', 1), ('SCR', 2)), dtag)
                st.append(s_r3)
            else:
                st.append(lambda: cp('act', QTn[:, c, :], src, (srctag,), dtag))
            st.append(lambda: qload(t, c))
            return st

        def qproj_chunk(t, c):
            for f_ in qproj_stages(t, c):
                f_()

        def qload(t, c):
            if c == 3:
                wload(WT[0], w_qkv[:, 512:1024], ('WT', 0))
            elif c == 7 and t + 1 < NT:
                wload(WT[0], w_qkv[:, 0:512], ('WT', 0))

        wload(WT[0], w_qkv[:, 0:512], ('WT', 0))
        for c in range(8):
            qproj_chunk(0, c)
        for t in range(NT):
            samp = t < 4
            cols = tsl(t)
            QT = QTB[t % 2]
            pairs = []
            if _SUB == 3:
                return
            qtag = tuple(('QT', t % 2, c) for c in range(8))
            ht = ('H', t)
            if isA:
                for gp in range(2):
                    cb = 4 * gp
                    if samp:
                        qlist = [(t * 4 + qb, qb * 128, None) for qb in range(4)]
                    else:
                        qlist = [(None, sq_ * 256 + qb * 128, sq_) for sq_ in range(2) for qb in range(2)]
                    for b, qc, sq_ in qlist:
                        jobs = []
                        for u in range(2):
                            g = 2 * gp + u
                            rows = slice(u * 64, u * 64 + 64)
                            vsl = slice(g * 66, g * 66 + 65)
                            kl = []
                            if samp:
                                for cxt in range(2):
                                    kc = K_CTX + cxt * 128
                                    kl.append((KB[rows, g // 2, kc:kc + 128], VB[:, kc // 128, vsl], None, ('KB', 'VB')))
                                if b == 0:
                                    kl.append((KB[rows, g // 2, K_HL:K_HL + 128], VB[:, K_HL // 128, vsl], MNEG[:, 2, :], ('KB', 'VB')))
                                else:
                                    kl.append((KB[rows, g // 2, (b - 1) * 128:b * 128], VB[:, b - 1, vsl], MNEG[:, 0, :], ('KB', 'VB')))
                                kl.append((KB[rows, g // 2, b * 128:(b + 1) * 128], VB[:, b, vsl], None, ('KB', 'VB')))
                                if b == 15:
                                    kl.append((KB[rows, g // 2, K_HR:K_HR + 128], VB[:, K_HR // 128, vsl], MNEG[:, 3, :], ('KB', 'VB')))
                                else:
                                    kl.append((KB[rows, g // 2, (b + 1) * 128:(b + 2) * 128], VB[:, b + 1, vsl], MNEG[:, 1, :], ('KB', 'VB')))
                            else:
                                for kk in range(2):
                                    kc = K_P + sq_ * 256 + kk * 128
                                    kl.append((KB[rows, g // 2, kc:kc + 128], VB[:, kc // 128, vsl], None, ('KB', 'VB')))
                            jobs.append(dict(kts=kl, rhs=QT[rows, cb:cb + 4, qc:qc + 128],
                                             sink=ESINK[64:65, j, g * 4:(g + 1) * 4],
                                             dest=H[rows, cb:cb + 4, t * 512 + qc:t * 512 + qc + 128], u=u, htag=ht))
                        pairs.append((jobs, 512, 4))
            else:
                for c in range(8):
                    if samp:
                        qlist = [(0, 512, None)]
                    else:
                        qlist = [(sq_ * 256, 256, sq_) for sq_ in range(2)]
                    for qc, qn, sq_ in qlist:
                        jobs = []
                        for u in range(2):
                            g = 2 * (c // 4) + u
                            rows = slice(u * 64, u * 64 + 64)
                            vsl = slice(g * 66, g * 66 + 65)
                            if samp:
                                kl = [(KB[rows, g // 2, kt * 128:(kt + 1) * 128], VB[:, kt, vsl], None, ('KB', 'VB')) for kt in range(34)]
                            else:
                                kl = []
                                for kk in range(2):
                                    kc = K_P + sq_ * 256 + kk * 128
                                    kl.append((KB[rows, g // 2, kc:kc + 128], VB[:, kc // 128, vsl], None, ('KB', 'VB')))
                            jobs.append(dict(kts=kl, rhs=QT[rows, c, qc:qc + qn], sink=None,
                                             dest=H[rows, c, t * 512 + qc:t * 512 + qc + qn], u=u, htag=ht))
                        pairs.append((jobs, qn, 1))
            for pi, (jobs, n_, ns_) in enumerate(pairs):
                side = qproj_stages(t + 1, pi) if (t + 1 < NT and pi < 8) else []
                attn_pair(jobs, n_, ns_, scale, qtag, PTb, ONBb, side)
                while side:
                    side.pop(0)()
            if t + 1 < NT:
                for c in range(len(pairs), 8):
                    qproj_chunk(t + 1, c)
            attn_flush()
        if _SUB == 4:
            attn_flush()
            return
        attn_flush()
        P.barrier()
        out_proj(l, w_o, 16)

    def mla_layer(l):
        NK, NKT = 4864, 38
        K_CTX, K_P = 4096, 4352
        off = 0
        CKV = arena(off, 2 * NK).rearrange("p (c n) -> p c n", c=2); off += 2 * NK
        KH = arena(off, NK); off += NK
        QL = arena(off, 3 * T).rearrange("p (c n) -> p c n", c=3); off += 3 * T
        VH = arena(off, NKT * 65).rearrange("p (t n) -> p t n", n=65); off += NKT * 65
        QH = arena(off, T); off += T
        WS = [arena(off + i * 544, 544) for i in range(2)]; off += 1088
        PTb = [arena(off + i * 512, 512) for i in range(3)]; off += 1536
        ONBb = arena(off, 512); off += 512
        WD = arena(2 * NK + NK + 3 * T, 8 * 672).rearrange("p (k n) -> p k n", k=8)
        scale = 96.0 ** -0.5
        wload(WD[:, :, 0:384], b_w_dq, 'WD')
        wload(WD[:, :, 384:672], b_w_dkv, 'WD')
        dma('pool', CKV[:, :, K_CTX:K_CTX + 256], ctx_b_ckv.rearrange("(c p) t -> p c t", p=128), (), ('CKV',), 'io0')
        dma('pool', KH[64:96, K_CTX:K_CTX + 256], ctx_b_kr, (), ('KH',), 'io1')
        for t in range(NT):
            samp = t < 4
            cols = tsl(t)
            kcol0 = t * 512 if samp else K_P
            if samp:
                dma('sp', ROPE[:, :, :], rope[2:4, :, cols].rearrange("a p n -> p a n"), (), ('ROPE',), 'io2')
            pbs = [ps[0], ps[1], ps[2]]
            for c in range(3):
                for k in range(8):
                    mm(pbs[c][:], WD[:, k, c * 128:(c + 1) * 128], H[:, k, cols], k == 0, k == 7, ('WD', ('H', t)), (('ps', id(pbs[c])),))
            r = rms_stats(lambda k: pbs[k][:], 3, 512, 1.0 / 384, ONESB[:], lambda k: (('ps', id(pbs[k])),), ps[3])
            for c in range(3):
                stt('dve', QL[:, c, cols], pbs[c][:], BG[:, c:c + 1], r, ALU.mult, ALU.mult,
                    (('ps', id(pbs[c])), 'BG', ('SCR', 0)), ('QL',))
            pbs2 = [ps[4], ps[5]]
            for c in range(2):
                for k in range(8):
                    mm(pbs2[c][:], WD[:, k, 384 + c * 128:384 + (c + 1) * 128], H[:, k, cols], k == 0, k == 7, ('WD', ('H', t)), (('ps', id(pbs2[c])),))
            r = rms_stats(lambda k: pbs2[k][:], 2, 512, 1.0 / 256, ONESB[:], lambda k: (('ps', id(pbs2[k])),), ps[3])
            for c in range(2):
                if samp:
                    stt('dve', CKV[:, c, kcol0:kcol0 + 512], pbs2[c][:], BG[:, 3 + c:4 + c], r, ALU.mult, ALU.mult,
                        (('ps', id(pbs2[c])), 'BG', ('SCR', 0)), ('CKV',))
                else:
                    cf = scr(3)
                    stt('dve', cf, pbs2[c][:], BG[:, 3 + c:4 + c], r, ALU.mult, ALU.mult,
                        (('ps', id(pbs2[c])), 'BG', ('SCR', 0)), (('SCR', 3),))
                    cp('act', CKV[:, c, kcol0:kcol0 + 512], cf, (('SCR', 3),), ('CKV',))
                    dma('sp', st_b_ckv[c * 128:(c + 1) * 128, :], cf, (('SCR', 3),), (), io_stream())
            pk = ps[7]
            for k in range(8):
                mm(pk[64:96, :], WD[:, k, 640:672], H[:, k, cols], k == 0, k == 7, ('WD', ('H', t)), (('ps', id(pk)),))
            if samp:
                rope_apply(pk[64:96, :], slice(64, 96), 512, ROPE[64:96, 0, :], ROPE[64:96, 1, :], PERMM[64:96, 0:32],
                           KH[64:96, kcol0:kcol0 + 512], ('ps', id(pk)), ('KH',))
            else:
                kf = SCR[64:96, 3, :]
                cp('dve', kf, pk[64:96, :], (('ps', id(pk)),), (('SCR', 3),))
                cp('act', KH[64:96, kcol0:kcol0 + 512], kf, (('SCR', 3),), ('KH',))
                dma('sp', st_b_kr[:, :], kf, (('SCR', 3),), (), io_stream())
        ki, ko = exK_in['B'], exK_out['B']
        for c in range(2):
            dma('sp', ki[c * 128:(c + 1) * 128, :], CKV[:, c, 0:TS], ('CKV',), ('exki',), f'ex{c}')
        dma('sp', exR_in[:, :], KH[64:96, 0:TS], ('KH',), ('exri',), 'ex2')
        P.add('pool', lambda e, a=ki, b=ko: e.collective_compute("AllGather", ALU.bypass, replica_groups=[[0, 1], [2, 3], [4, 5], [6, 7]],
                                                                 ins=[a.ap().opt()], outs=[b.ap().opt()]),
              ('exki',), ('exko',), kind='cc', stream=f'cc{l}k')
        P.add('pool', lambda e, a=exR_in, b=exR_out: e.collective_compute("AllGather", ALU.bypass, replica_groups=[[0, 1], [2, 3], [4, 5], [6, 7]],
                                                                           ins=[a.ap().opt()], outs=[b.ap().opt()]),
              ('exri',), ('exro',), kind='cc', stream=f'cc{l}r')
        for r_ in range(2):
            for c in range(2):
                dma('sp', CKV[:, c, r_ * TS:(r_ + 1) * TS], ko[r_ * 256 + c * 128:r_ * 256 + (c + 1) * 128, :], ('exko',), ('CKV',), f'ex{c}')
            dma('sp', KH[64:96, r_ * TS:(r_ + 1) * TS], exR_out[r_ * 32:(r_ + 1) * 32, :], ('exro',), ('KH',), 'ex2')
        P.barrier()
        memset('dve', VH[:], 1.0, ('VH',))
        for h in range(16):
            ws = WS[h % 2]
            wtag = ('WS', h % 2)
            WUQ = ws[:, 0:288].rearrange("p (k n) -> p k n", k=3)
            WUKV = ws[:, 288:544].rearrange("p (k n) -> p k n", k=2)
            wload(WUQ, b_w_uq[:, h * 96:(h + 1) * 96], wtag)
            wload(WUKV, b_w_ukv[:, h * 128:(h + 1) * 128], wtag)
            for kc in range(0, NK, 512):
                n = min(512, NK - kc)
                pb = rotate('proj', [ps[6], ps[7]])
                ptag = ('ps', id(pb))
                for k in range(2):
                    mm(pb[0:64, 0:n], WUKV[:, k, 0:64], CKV[:, k, kc:kc + n], k == 0, k == 1, (wtag, 'CKV'), (ptag,))
                cp('dve', KH[0:64, kc:kc + n], pb[0:64, 0:n], (ptag,), ('KH',))
            for k0 in range(0, NKT, 8):
                nk = min(8, NKT - k0)
                pb = rotate('proj', [ps[6], ps[7]])
                ptag = ('ps', id(pb))
                for i in range(nk):
                    kt = k0 + i
                    for k in range(2):
                        mm(pb[:, i * 64:(i + 1) * 64], CKV[:, k, kt * 128:(kt + 1) * 128], WUKV[:, k, 64:128], k == 0, k == 1,
                           (wtag, 'CKV'), (ptag,))
                cp('dve', VH[:, k0:k0 + nk, 0:64], pb[:, 0:nk * 64].rearrange("p (t d) -> p t d", d=64), (ptag,), ('VH',))
            for t in range(NT):
                samp = t < 4
                cols = tsl(t)
                pb = rotate('proj', [ps[6], ps[7]])
                ptag = ('ps', id(pb))
                for k in range(3):
                    mm(pb[0:96, :], WUQ[:, k, :], QL[:, k, cols], k == 0, k == 2, (wtag, 'QL'), (ptag,))
                cp('act', QH[0:64, cols], pb[0:64, :], (ptag,), ('QH',))
                if samp:
                    dma('sp', ROPE[:, :, :], rope[2:4, :, cols].rearrange("a p n -> p a n"), (), ('ROPE',), 'io2')
                    rope_apply(pb[64:96, :], slice(64, 96), 512, ROPE[64:96, 0, :], ROPE[64:96, 1, :], PERMM[64:96, 0:32],
                               QH[64:96, cols], ptag, ('QH',))
                else:
                    cp('act', QH[64:96, cols], pb[64:96, :], (ptag,), ('QH',))
            u = h % 2
            rows = slice(u * 64, u * 64 + 64)
            for t in range(NT):
                cols = tsl(t)
                if t < 4:
                    kl = [(KH[0:96, kt * 128:(kt + 1) * 128], VH[:, kt, :], None, ('KH', 'VH')) for kt in range(34)]
                    attn_job(kl, QH[0:96, cols], 512, 1, scale, None, H[rows, h // 2, cols], u, ('QH',), PTb, ONBb, ('H', t))
                else:
                    for sq_ in range(2):
                        kl = []
                        for kk in range(2):
                            kc = K_P + sq_ * 256 + kk * 128
                            kl.append((KH[0:96, kc:kc + 128], VH[:, kc // 128, :], None, ('KH', 'VH')))
                        qc = t * 512 + sq_ * 256
                        attn_job(kl, QH[0:96, qc:qc + 256], 256, 1, scale, None, H[rows, h // 2, qc:qc + 256], u, ('QH',), PTb, ONBb, ('H', t))
        attn_flush()
        P.barrier()
        out_proj(l, b_w_o, 16)

    for l in range(nlayers):
        kind = 'ABC'[l % 3]
        j = l // 3
        last = l == nlayers - 1
        if last and stop < 2:
            break
        if l == 0:
            ada_mods(l)
            P.barrier()
        norm_phase(0, l)
        if last and stop < 3:
            break
        if kind == 'A':
            gqa_layer(l, 'A', j, a_w_qkv[j], a_w_o[j], ctx_a_k[j], ctx_a_v[j], st_a_k[j], st_a_v[j])
        elif kind == 'B':
            mla_layer(l)
        else:
            gqa_layer(l, 'C', 0, c_w_qkv, c_w_o, ctx_c_k, ctx_c_v, st_c_k, st_c_v)
        P.barrier()
        if last and stop < 4:
            break
        norm_phase(1, l)
        mlp(l)
        P.barrier()
    for t in range(NT):
        norm_tile(t, 0, 0, out_final=True)

    P.analyze()
    sems = {}
    for e in Prog.ENGS:
        n = (P.nsig[e] + CH - 1) // CH
        sems[e] = [es.enter_context(nc.semaphore(f"s_{e}{i}")) for i in range(max(n, 1))]
    ssem = {s: es.enter_context(nc.semaphore(f"d_{s}")) for s in P.stream_cnt}

    def emit(e, eng):
        for o in P.eng_ops[e]:
            for d in o.cwaits:
                k = d.sigk - 1
                eng.wait_ge(sems[d.eng][k // CH], k % CH + 1)
            for d in o.swaits:
                eng.wait_ge(ssem[d.stream], 1 if d.kind == 'cc' else 16 * d.dman)
            if o.fn is None:
                continue
            ins = o.fn(eng)
            if o.kind == 'dma':
                ins.then_inc(ssem[o.stream], 16)
            elif o.kind == 'cc':
                ins.then_inc(ssem[o.stream])
            elif o.sig:
                k = o.sigk - 1
                ins.then_inc(sems[e][k // CH], 1)
        if e == 'sp':
            for s, n in P.stream_cnt.items():
                if not s.startswith('cc'):
                    eng.wait_ge(ssem[s], 16 * n)

    with nc.Block() as block:
        @block.tensor
        def _(eng):
            emit('pe', eng)

        @block.scalar
        def _(eng):
            emit('act', eng)

        @block.vector
        def _(eng):
            emit('dve', eng)

        @block.gpsimd
        def _(eng):
            emit('pool', eng)

        @block.sync
        def _(eng):
            emit('sp', eng)
    es.close()
    return nc


def _rope_tables(hf):
    t = np.arange(TS) + hf * TS
    rows = (t // 64).astype(np.float64)
    colp = (t % 64).astype(np.float64)
    out = np.zeros((4, 128, TS), np.float32)

    def fill(cos_t, sin_t, p0, nd, pos):
        hlf = nd // 2
        fr = THETA ** (-np.arange(0, nd, 2, dtype=np.float32) / nd)
        ang = (pos.astype(np.float32)[None, :] * fr[:, None].astype(np.float32)).astype(np.float32)
        c, s = np.cos(ang), np.sin(ang)
        cos_t[p0:p0 + hlf] = c
        cos_t[p0 + hlf:p0 + nd] = c
        sin_t[p0:p0 + hlf] = -s
        sin_t[p0 + hlf:p0 + nd] = s
    for base in (0, 64):
        fill(out[0], out[1], base, 32, rows)
        fill(out[0], out[1], base + 32, 32, colp)
    fill(out[2], out[3], 64, 16, rows)
    fill(out[2], out[3], 80, 16, colp)
    return out


def _consts():
    c = np.zeros((128, 6, 128), np.float32)
    c[:, 0, :] = np.eye(128)
    for base in (0, 32, 64, 96):
        for i in range(16):
            c[base + i + 16, 1, base + i] = 1.0
            c[base + i, 1, base + i + 16] = 1.0
    for base in (0, 16):
        for i in range(8):
            c[64 + base + i + 8, 2, base + i] = 1.0
            c[64 + base + i, 2, base + i + 8] = 1.0
    c[0:64, 3, 0:64] = 1.0
    c[64:128, 3, 64:128] = 1.0
    return c


def _masks(hf):
    k = np.arange(128)[:, None]
    q = np.arange(128)[None, :]
    m = np.zeros((128, 4, 128), np.float32)
    m[:, 0, :] = (k >= q)
    m[:, 1, :] = (k <= q)
    if hf == 1:
        m[:, 2, :] = (k >= q)
    if hf == 0:
        m[:, 3, :] = (k <= q)
    return m


def _qperm():
    idx = np.zeros(1024, np.int64)
    for c in range(8):
        for u in range(2):
            g = 2 * (c // 4) + u
            i = c % 4
            h = g * 4 + i
            idx[c * 128 + u * 64:c * 128 + u * 64 + 64] = h * 64 + np.arange(64)
    return idx


def _fm(v):
    return np.ascontiguousarray(np.asarray(v, np.float32).reshape(-1, 128).T)


_NC = None
_DBG = (DEPTH, 9)
_SUB = 0
_RL = 9


def kernel(x_prompt, x_sample, c, cache_a_k, cache_a_v, cache_b_ckv, cache_b_krope, cache_c_k, cache_c_v,
           c_ctx, w_ada, b_ada, norm_g, w_mlp_in, w_mlp_out, a_w_qkv, a_sink, a_w_o,
           b_w_dq, b_g_q, b_w_uq, b_w_dkv, b_g_kv, b_w_ukv, b_w_o, c_w_qkv, c_g_q, c_g_k, c_w_o, g_final):
    global _NC
    in_maps = _prepare(x_prompt, x_sample, c, cache_a_k, cache_a_v, cache_b_ckv, cache_b_krope, cache_c_k, cache_c_v,
                       c_ctx, w_ada, b_ada, norm_g, w_mlp_in, w_mlp_out, a_w_qkv, a_sink, a_w_o,
                       b_w_dq, b_g_q, b_w_uq, b_w_dkv, b_g_kv, b_w_ukv, b_w_o, c_w_qkv, c_g_q, c_g_k, c_w_o, g_final)
    if _NC is None:
        _NC = build_program(*_DBG)
    res = run_bass_kernel_spmd(_NC, in_maps, core_ids=list(range(8))).results
    return _assemble(res)


def _prepare(x_prompt, x_sample, c, cache_a_k, cache_a_v, cache_b_ckv, cache_b_krope, cache_c_k, cache_c_v,
             c_ctx, w_ada, b_ada, norm_g, w_mlp_in, w_mlp_out, a_w_qkv, a_sink, a_w_o,
             b_w_dq, b_g_q, b_w_uq, b_w_dkv, b_g_kv, b_w_ukv, b_w_o, c_w_qkv, c_g_q, c_g_k, c_w_o, g_final):
    f = lambda a: np.ascontiguousarray(np.asarray(a, np.float32))
    x_prompt, x_sample, c = f(x_prompt), f(x_sample), f(c)
    qp = _qperm()
    qkvp = np.concatenate([qp, 1024 + np.arange(512)])
    a_qkv_p = f(np.asarray(a_w_qkv)[:, :, qkvp])
    a_o_p = f(np.asarray(a_w_o)[:, qp, :])
    c_qkv_p = f(np.asarray(c_w_qkv)[0][:, qkvp])
    c_o_p = f(np.asarray(c_w_o)[0][qp, :])
    shared = {
        "consts": _consts(),
        "w_ada": f(np.asarray(w_ada)[:_DBG[0]]),
        "b_adaT": f(np.asarray(b_ada).reshape(DEPTH, 48, 128).transpose(2, 0, 1)),
        "ngT": f(np.concatenate([np.asarray(norm_g).reshape(DEPTH * 2, 8, 128), np.asarray(g_final).reshape(1, 8, 128)], 0).transpose(2, 0, 1)),
        "w_mlp_in": f(np.asarray(w_mlp_in)[:_DBG[0]]), "w_mlp_out": f(np.asarray(w_mlp_out)[:_DBG[0]]),
        "a_w_qkv": a_qkv_p, "a_w_o": a_o_p,
        "a_sinkb": f(np.broadcast_to(np.asarray(a_sink)[None], (128, 2, 16))),
        "b_w_dq": f(np.asarray(b_w_dq)[0]), "b_w_uq": f(np.asarray(b_w_uq)[0]), "b_w_dkv": f(np.asarray(b_w_dkv)[0]),
        "b_w_ukv": f(np.asarray(b_w_ukv)[0]), "b_w_o": f(np.asarray(b_w_o)[0]),
        "b_gT": f(np.concatenate([_fm(np.asarray(b_g_q)[0]), _fm(np.asarray(b_g_kv)[0])], 1)),
        "c_w_qkv": c_qkv_p, "c_w_o": c_o_p,
        "c_gT": f(np.stack([np.tile(np.asarray(c_g_q)[0], 2), np.tile(np.asarray(c_g_k)[0], 2)], 1)),
    }
    in_maps = []
    for core in range(8):
        p, hf = core // 2, core % 2
        xs = x_sample[p, hf * TS:(hf + 1) * TS]
        xt = np.concatenate([xs, x_prompt[2 * core], x_prompt[2 * core + 1]], 0).T
        m = dict(shared)
        m["xT"] = f(xt)
        m["cvec"] = f(np.stack([_fm(c_ctx), _fm(c[p])], 2))
        m["rope"] = _rope_tables(hf)
        m["masks"] = _masks(hf)
        m["ctx_a_k"] = f(np.asarray(cache_a_k)[p].reshape(2, 256, 256).transpose(0, 2, 1))
        m["ctx_a_v"] = f(np.asarray(cache_a_v)[p].reshape(2, 256, 256))
        m["ctx_b_ckv"] = f(np.asarray(cache_b_ckv)[p, 0].T)
        m["ctx_b_kr"] = f(np.asarray(cache_b_krope)[p, 0].T)
        m["ctx_c_k"] = f(np.asarray(cache_c_k)[p, 0].reshape(256, 256).T)
        m["ctx_c_v"] = f(np.asarray(cache_c_v)[p, 0].reshape(256, 256))
        in_maps.append(m)
    return in_maps


def _assemble(res):
    y_prompt = np.zeros((16, 256, D), np.float32)
    y_sample = np.zeros((4, 4096, D), np.float32)
    s_a_k = np.zeros((16, 2, 256, 4, 64), np.float32)
    s_a_v = np.zeros((16, 2, 256, 4, 64), np.float32)
    s_b_c = np.zeros((16, 1, 256, 256), np.float32)
    s_b_r = np.zeros((16, 1, 256, 32), np.float32)
    s_c_k = np.zeros((16, 1, 256, 4, 64), np.float32)
    s_c_v = np.zeros((16, 1, 256, 4, 64), np.float32)
    for core in range(8):
        r = res[core]
        p, hf = core // 2, core % 2
        y = r["yT"].T
        y_sample[p, hf * TS:(hf + 1) * TS] = y[0:TS]
        for s in range(2):
            b = 2 * core + s
            y_prompt[b] = y[TS + s * 256:TS + (s + 1) * 256]
            sl = slice(s * 256, (s + 1) * 256)
            for jj in range(2):
                s_a_k[b, jj] = r["st_a_k"][jj][:, sl].T.reshape(256, 4, 64)
                s_a_v[b, jj] = r["st_a_v"][jj][sl].reshape(256, 4, 64)
            s_b_c[b, 0] = r["st_b_ckv"][:, sl].T
            s_b_r[b, 0] = r["st_b_kr"][:, sl].T
            s_c_k[b, 0] = r["st_c_k"][:, sl].T.reshape(256, 4, 64)
            s_c_v[b, 0] = r["st_c_v"][sl].reshape(256, 4, 64)
    return (y_prompt, y_sample, s_a_k, s_a_v, s_b_c, s_b_r, s_c_k, s_c_v)
```
